# Optimizing a Trainium2 kernel written in Bass

```python
import jax, jax.numpy as jnp
from jax import lax
import numpy as np

D_MODEL = 1024
BATCH = 8
SEQ = 8192
DEPTH = 1

CHUNK = 64
DN_HEADS = 8
DN_HEAD_DIM = 128
DN_WIDTH = DN_HEADS * DN_HEAD_DIM
DN_CONV = 4
CM_WIDTH = D_MODEL
CM_KERNEL = 31
N_BRANCH = 2
FFN_HIDDEN = -(-8 * D_MODEL // (3 * 256)) * 256
EPS = 1e-6
SPLITS = (DN_WIDTH, 2 * DN_WIDTH, 3 * DN_WIDTH, 4 * DN_WIDTH,
          4 * DN_WIDTH + DN_HEADS, 4 * DN_WIDTH + 2 * DN_HEADS,
          4 * DN_WIDTH + 2 * DN_HEADS + 2 * CM_WIDTH)
IN_COLS = 4 * DN_WIDTH + 2 * DN_HEADS + 2 * CM_WIDTH + N_BRANCH * D_MODEL

kernel_name = "hybrid_gdn_conformer_gated_block"


def rmsnorm(x, w):
    x32 = x.astype(jnp.float32)
    y = x32 * lax.rsqrt(jnp.mean(x32 * x32, axis=-1, keepdims=True) + EPS)
    return y.astype(x.dtype) * w


def layernorm(x, w, b):
    x32 = x.astype(jnp.float32)
    mu = jnp.mean(x32, axis=-1, keepdims=True)
    xc = x32 - mu
    y = xc * lax.rsqrt(jnp.mean(xc * xc, axis=-1, keepdims=True) + EPS)
    return y.astype(x.dtype) * w + b


def l2norm(x):
    return x * lax.rsqrt(jnp.sum(x * x, axis=-1, keepdims=True) + EPS)


def causal_depthwise_conv(x, w):
    K = w.shape[0]
    return lax.conv_general_dilated(
        x, w[:, None, :], window_strides=(1,), padding=[(K - 1, 0)],
        dimension_numbers=("NWC", "WIO", "NWC"), feature_group_count=x.shape[-1])


def gated_delta_rule_chunked(q, k, v, g, beta):
    q, k, v, g, beta = (t.astype(jnp.float32) for t in (q, k, v, g, beta))
    B, T, H, DK = q.shape
    DV = v.shape[-1]
    N = T // CHUNK
    chunk = lambda t: jnp.moveaxis(t.reshape(B, N, CHUNK, H, *t.shape[3:]), 3, 2)
    q = chunk(q) * (DK ** -0.5)
    k, v, g, beta = chunk(k), chunk(v), chunk(g), chunk(beta)
    gc = jnp.cumsum(g, axis=-1)
    idx = jnp.arange(CHUNK)
    tril = idx[:, None] >= idx[None, :]
    strict = idx[:, None] > idx[None, :]
    decay = jnp.exp(jnp.where(tril, gc[..., :, None] - gc[..., None, :], -jnp.inf))
    kk = jnp.einsum("bnhid,bnhjd->bnhij", k, k)
    m_low = jnp.where(strict, beta[..., :, None] * kk * decay, 0.0)
    t_mat = m_low + jnp.eye(CHUNK, dtype=jnp.float32)
    rhs = jnp.concatenate([v * beta[..., None], k * (beta * jnp.exp(gc))[..., None]], axis=-1)
    sol = lax.linalg.triangular_solve(t_mat, rhs, left_side=True, lower=True, unit_diagonal=True)
    u, w = sol[..., :DV], sol[..., DV:]
    attn = jnp.einsum("bnhid,bnhjd->bnhij", q, k) * decay
    q_dec = q * jnp.exp(gc)[..., None]
    g_last = gc[..., -1]
    k_dec = k * jnp.exp(g_last[..., None] - gc)[..., None]
    xs = tuple(jnp.moveaxis(t, 1, 0) for t in (u, w, attn, q_dec, k_dec, jnp.exp(g_last)))

    def step(S, xs_c):
        u_c, w_c, a_c, qd_c, kd_c, el_c = xs_c
        v_new = u_c - jnp.einsum("bhcd,bhde->bhce", w_c, S)
        o_c = jnp.einsum("bhcd,bhde->bhce", qd_c, S) + jnp.einsum("bhij,bhje->bhie", a_c, v_new)
        S = S * el_c[..., None, None] + jnp.einsum("bhcd,bhce->bhde", kd_c, v_new)
        return S, o_c

    S0 = jnp.zeros((B, H, DK, DV), jnp.float32)
    _, o = lax.scan(step, S0, xs)
    o = jnp.moveaxis(o, 0, 1)
    return jnp.moveaxis(o, 2, 3).reshape(B, T, H, DV)


def setup_inputs(seed: int = 0) -> dict:
    key = jax.random.key(seed)
    ks = jax.random.split(key, 24)
    L = DEPTH
    nrm = lambda k, shape, fan_in: jax.random.normal(k, shape, jnp.float32) * (fan_in ** -0.5)
    gain = lambda k, shape: 1.0 + 0.02 * jax.random.normal(k, shape, jnp.float32)
    small = lambda k, shape: 0.01 * jax.random.normal(k, shape, jnp.float32)
    dt = jnp.exp(jax.random.uniform(ks[7], (L, DN_HEADS), jnp.float32, np.log(1e-3), np.log(1e-1)))
    return {
        "x": jax.random.normal(ks[0], (BATCH, SEQ, D_MODEL), jnp.float32),
        "norm1_w": gain(ks[1], (L, D_MODEL)),
        "w_in": nrm(ks[2], (L, D_MODEL, IN_COLS), D_MODEL),
        "b_glu": small(ks[3], (L, 2 * CM_WIDTH)),
        "b_gates": small(ks[4], (L, N_BRANCH * D_MODEL)),
        "dn_conv_w": nrm(ks[5], (L, DN_CONV, 3 * DN_WIDTH), DN_CONV),
        "dn_A_log": jnp.log(jax.random.uniform(ks[6], (L, DN_HEADS), jnp.float32, 1.0, 16.0)),
        "dn_dt_bias": dt + jnp.log(-jnp.expm1(-dt)),
        "dn_norm_w": gain(ks[8], (L, DN_HEAD_DIM)),
        "dn_w_o": nrm(ks[9], (L, DN_WIDTH, D_MODEL), DN_WIDTH),
        "cm_dw_w": nrm(ks[10], (L, CM_KERNEL, CM_WIDTH), CM_KERNEL),
        "cm_dw_b": small(ks[11], (L, CM_WIDTH)),
        "cm_ln_w": gain(ks[12], (L, CM_WIDTH)),
        "cm_ln_b": small(ks[13], (L, CM_WIDTH)),
        "cm_w_pw2": nrm(ks[14], (L, CM_WIDTH, D_MODEL), CM_WIDTH),
        "cm_b_pw2": small(ks[15], (L, D_MODEL)),
        "w_out": nrm(ks[16], (L, D_MODEL, D_MODEL), D_MODEL),
        "norm2_w": gain(ks[17], (L, D_MODEL)),
        "ffn_w_gate_up": nrm(ks[18], (L, D_MODEL, 2 * FFN_HIDDEN), D_MODEL),
        "ffn_w_down": nrm(ks[19], (L, FFN_HIDDEN, D_MODEL), FFN_HIDDEN),
        "norm_f_w": gain(ks[20], (D_MODEL,)),
    }


def reference(x, norm1_w, w_in, b_glu, b_gates, dn_conv_w, dn_A_log, dn_dt_bias, dn_norm_w,
              dn_w_o, cm_dw_w, cm_dw_b, cm_ln_w, cm_ln_b, cm_w_pw2, cm_b_pw2, w_out,
              norm2_w, ffn_w_gate_up, ffn_w_down, norm_f_w):
    B, T, _ = x.shape
    for l in range(DEPTH):
        h = rmsnorm(x, norm1_w[l])
        proj = h @ w_in[l]
        q, k, v, z, b_raw, a_raw, glu, gates = jnp.split(proj, SPLITS, axis=-1)

        qkv = jax.nn.silu(causal_depthwise_conv(jnp.concatenate([q, k, v], axis=-1), dn_conv_w[l]))
        q, k, v = (t.reshape(B, T, DN_HEADS, DN_HEAD_DIM) for t in jnp.split(qkv, 3, axis=-1))
        q, k = l2norm(q.astype(jnp.float32)), l2norm(k.astype(jnp.float32))
        beta = jax.nn.sigmoid(b_raw.astype(jnp.float32))
        g = -jnp.exp(dn_A_log[l].astype(jnp.float32)) * jax.nn.softplus(
            a_raw.astype(jnp.float32) + dn_dt_bias[l].astype(jnp.float32))
        o = gated_delta_rule_chunked(q, k, v, g, beta).astype(x.dtype)
        o = rmsnorm(o, dn_norm_w[l]) * jax.nn.silu(z.reshape(B, T, DN_HEADS, DN_HEAD_DIM))
        out_a = o.reshape(B, T, DN_WIDTH) @ dn_w_o[l]

        ga, gb = jnp.split(glu + b_glu[l], 2, axis=-1)
        c = ga * jax.nn.sigmoid(gb)
        c = causal_depthwise_conv(c, cm_dw_w[l]) + cm_dw_b[l]
        c = jax.nn.silu(layernorm(c, cm_ln_w[l], cm_ln_b[l]))
        out_b = c @ cm_w_pw2[l] + cm_b_pw2[l]

        gate_a, gate_b = jnp.split(jax.nn.sigmoid(gates + b_gates[l]), 2, axis=-1)
        x = x + (gate_a * out_a + gate_b * out_b) @ w_out[l]

        h2 = rmsnorm(x, norm2_w[l])
        f_gate, f_up = jnp.split(h2 @ ffn_w_gate_up[l], 2, axis=-1)
        x = x + (jax.nn.silu(f_gate) * f_up) @ ffn_w_down[l]
    return rmsnorm(x, norm_f_w)
```

```python
import numpy as np
from contextlib import ExitStack
import concourse.bass as bass
import concourse.mybir as mybir
from concourse.bass_utils import run_bass_kernel_spmd

F32 = mybir.dt.float32
BF16 = mybir.dt.bfloat16
AF = mybir.ActivationFunctionType
ALU = mybir.AluOpType

D = 1024
NH = 8
INC = 8208
FH = 2816
EPS = 1e-6
BIG = 30000.0


class Op:
    __slots__ = ("eng", "fn", "deps", "flag", "val", "dsem", "dval", "idx")


class Prog:
    def __init__(self, nc):
        self.nc = nc
        self.ops = []
        self.last_w = {}
        self.readers = {}
        self.dma_tot = {}
        self.dma_serial = {}

    def add(self, eng, fn, reads=(), writes=(), dma=None, serial=True):
        op = Op()
        op.eng = eng
        op.fn = fn
        op.flag = False
        op.val = None
        op.dsem = dma
        op.dval = None
        op.idx = len(self.ops)
        psr = [k for k in reads if isinstance(k, tuple) and k[0] == "ps"]
        if psr:
            reads = [k for k in reads if not (isinstance(k, tuple) and k[0] == "ps")]
            writes = list(writes) + psr
        deps = {}
        for k in reads:
            w = self.last_w.get(k)
            if w is not None:
                deps[w.idx] = w
        for k in writes:
            w = self.last_w.get(k)
            if w is not None:
                deps[w.idx] = w
            for r in self.readers.get(k, {}).values():
                deps[r.idx] = r
        for k in reads:
            self.readers.setdefault(k, {})[(eng, dma)] = op
        for k in writes:
            self.last_w[k] = op
            self.readers[k] = {}
        out = []
        for d in deps.values():
            if d.dsem is None and d.eng == "pe" and eng == "pe" and dma is None:
                continue
            ov = None
            if d.dsem is not None and not self.dma_serial[d.dsem]:
                ov = self.dma_tot[d.dsem]
            out.append((d, ov))
        op.deps = out
        if dma is not None:
            self.dma_tot[dma] = self.dma_tot.get(dma, 0) + 16
            self.dma_serial[dma] = serial
            op.dval = self.dma_tot[dma]
        self.ops.append(op)
        return op

    def emit(self, sems, dsems):
        nc = self.nc
        for op in self.ops:
            for d, _ in op.deps:
                d.flag = True
        cnt = {}
        for op in self.ops:
            if op.dsem is None and op.flag:
                cnt[op.eng] = cnt.get(op.eng, 0) + 1
                op.val = cnt[op.eng]
        self.max_sem = dict(cnt)
        per = {}
        for op in self.ops:
            per.setdefault(op.eng, []).append(op)
        engmap = {"pe": "tensor", "act": "scalar", "dve": "vector", "pool": "gpsimd", "sp": "sync"}
        with nc.Block() as block:
            for en, lst in per.items():
                def body(e, lst=lst):
                    waited = {}
                    for op in lst:
                        need = {}
                        for d, ov in op.deps:
                            if d.dsem is not None:
                                s = dsems[d.dsem]
                                v = d.dval if ov is None else ov
                                key = ("d", d.dsem)
                            else:
                                s = sems[d.eng]
                                v = d.val
                                key = ("e", d.eng)
                            if waited.get(key, 0) >= v:
                                continue
                            if key not in need or need[key][1] < v:
                                need[key] = (s, v)
                        for key, (s, v) in need.items():
                            e.wait_ge(s, v)
                            waited[key] = v
                        ins = op.fn(e)
                        if op.dsem is not None:
                            ins.then_inc(dsems[op.dsem], 16)
                        elif op.flag:
                            ins.then_inc(sems[op.eng], 1)
                getattr(block, engmap[en])(body)


R_N1, R_BGLU, R_BGAT, R_DWB, R_LNW, R_LNB, R_BPW2, R_N2, R_DW, R_DNC = 0, 1, 3, 5, 6, 7, 8, 9, 10, 41
NROWS = 53


class _Stop(Exception):
    pass


def build(T, NSUB=4, dbg=False, limit=None):
    def stage(n):
        if limit is not None and n >= limit:
            raise _Stop()
    try:
        return _build(T, NSUB, stage)
    except _Stop:
        raise RuntimeError("unreachable")


def _build(T, NSUB, stage):
    import os
    SKIP = os.environ.get("SKIP", "")
    TT = 128 * NSUB
    NT = T // TT
    assert NT * TT == T
    nc = bass.Bass("TRN2", target_bir_lowering=False)
    P = Prog(nc)
    es = ExitStack()

    def din(name, shape):
        return nc.dram_tensor(name, list(shape), F32, kind="ExternalInput").ap()

    x = din("x", [T, D])
    norm1_w = din("norm1_w", [D])
    w_in = din("w_in", [D, INC])
    b_glu = din("b_glu", [2 * D])
    b_gates = din("b_gates", [2 * D])
    dn_conv_w = din("dn_conv_w", [4, 3 * D])
    dn_A_log = din("dn_A_log", [NH])
    dn_dt_bias = din("dn_dt_bias", [NH])
    dn_norm_w = din("dn_norm_w", [128])
    dn_w_o = din("dn_w_o", [D, D])
    cm_dw_w = din("cm_dw_w", [31, D])
    cm_dw_b = din("cm_dw_b", [D])
    cm_ln_w = din("cm_ln_w", [D])
    cm_ln_b = din("cm_ln_b", [D])
    cm_w_pw2 = din("cm_w_pw2", [D, D])
    cm_b_pw2 = din("cm_b_pw2", [D])
    w_out = din("w_out", [D, D])
    norm2_w = din("norm2_w", [D])
    ffn_gu = din("ffn_w_gate_up", [D, 2 * FH])
    ffn_dn = din("ffn_w_down", [FH, D])
    norm_f_w = din("norm_f_w", [D])
    out = nc.dram_tensor("out", [T, D], F32, kind="ExternalOutput").ap()

    def dscr(name, shape):
        return nc.dram_tensor(name, list(shape), BF16, kind="Internal").ap()

    wb_in = dscr("wb_in", [D, INC])
    wb_o = dscr("wb_o", [D, D])
    wb_pw2 = dscr("wb_pw2", [D, D])
    wb_out = dscr("wb_out", [D, D])
    wb_gu = dscr("wb_gu", [D, 2 * FH])
    wb_dn = dscr("wb_dn", [FH, D])

    def sb(name, shape, dt=F32):
        return es.enter_context(nc.sbuf_tensor(name, list(shape), dt))

    ident_b = sb("ident_b", [128, 128], BF16)
    ones_b = sb("ones_b", [128, 128], BF16)
    ones_f = sb("ones_f", [128, 128])
    idf = sb("idf", [128, 128])
    tri_f = sb("tri_f", [128, 128])
    chk_f = sb("chk_f", [128, 128])
    neg_t = sb("neg_t", [128, 128])
    pos_s = sb("pos_s", [128, 128])
    par = sb("par", [128, 8, NROWS])
    dnw = sb("dnw", [128, 1])
    dtb = sb("dtb", [128, NH])
    nega = sb("nega", [128, NH])
    nfw = sb("nfw", [128, D])
    wbd = sb("wbd", [128, 8, 16], BF16)

    XIN = [sb("xin%d" % s, [128, D]) for s in range(NSUB)]
    XS = [sb("xs%d" % s, [128, D]) for s in range(NSUB)]
    HB = [sb("hb%d" % i, [128, D], BF16) for i in range(2)]
    HT = sb("ht", [128, 8, TT], BF16)
    STT_ = sb("stt", [128, 8])
    PRE = [sb("pre%d" % i, [128, TT + 4], BF16) for i in range(2)]
    HALO = sb("halo", [128, 24, 4], BF16)
    NDG = 12
    DG = [sb("dg%d" % i, [128, 128], BF16) for i in range(NDG)]
    QKV = sb("qkv", [128, 24, TT], BF16)
    QN = QKV[:, 0:8, :]
    KN = QKV[:, 8:16, :]
    VS = QKV[:, 16:24, :]
    FF = QKV[:, 0:22, :]
    ZS = sb("zs", [128, 8, TT], BF16)
    CIN = sb("cin", [128, 8, 30 + TT], BF16)
    CN = CIN[:, :, 30:30 + TT]
    OG = sb("og", [128, 8, TT], BF16)
    MRG = ZS
    MA = [sb("ma%d" % i, [128, TT]) for i in range(2)]
    TMPW = [sb("tmpw%d" % i, [128, TT]) for i in range(2)]
    SQW = [sb("sqw%d" % i, [128, TT], BF16) for i in range(2)]
    SQL = [sb("sql%d" % i, [128, TT], BF16) for i in range(2)]
    SM = sb("sm", [128, NSUB, 12, NH])
    DSC = sb("dsc", [128, 4096])
    dsc3 = lambda i: DSC[:, i * 1024:(i + 1) * 1024].rearrange("p (h i) -> p h i", h=8)
    TRIG, EGB, XX, XA = dsc3(0), dsc3(1), dsc3(2), dsc3(3)
    CCV = DSC[:, 0:8 * TT].rearrange("p (c t) -> p c t", c=8)
    GATES = DSC[:, :].bitcast(BF16)[:, 0:16 * TT].rearrange("p (c t) -> p c t", c=16)
    RKN = ["trig", "egb", "xx", "xa"]

    def rk_ccv(c):
        names = sorted({RKN[(c * TT) // 1024], RKN[((c + 1) * TT - 1) // 1024]})
        return [(n, hs) for n in names for hs in (0, 1)]

    def rk_gates(c):
        return [(RKN[(c * TT // 2) // 1024], hs) for hs in (0, 1)]
    XB = sb("xb", [128, 8, 128])
    MMc = [sb("mm%d" % i, [128, 8, 128], BF16) for i in range(2)]
    NNc = [sb("nn%d" % i, [128, 8, 128], BF16) for i in range(2)]
    PF = sb("pf", [128, 8, 128])
    pfb = PF[:, :, :].rearrange("p h i -> p (h i)").bitcast(BF16)
    PBF = sb("pbf", [128, 8, 128], BF16)
    KBG = pfb[:, 0:1024].rearrange("p (h i) -> p h i", h=8)
    KD = pfb[:, 1024:2048].rearrange("p (h i) -> p h i", h=8)
    VB = sb("vb", [128, 8, 128], BF16)
    WT = sb("wt", [128, 8, 128], BF16)
    U = XB
    ATT = sb("att", [128, 8, 128], BF16)
    QD = sb("qd", [128, 8, 128], BF16)
    VN = sb("vn", [128, 8, 128], BF16)
    SQ = DSC[:, 0:512].bitcast(BF16).rearrange("p (h i) -> p h i", h=8)
    RR = TRIG
    T1 = WT
    S = sb("s", [128, 8, 128])
    SBF = sb("sbf", [128, 8, 128], BF16)
    NSLOT = 4
    WR = [sb("wr%d" % i, [128, 8, 512], BF16) for i in range(NSLOT)]
    rows = WR[3][:, :, :].rearrange("p k c -> p (k c)").bitcast(F32)[0:64, 0:D]

    PW = [es.enter_context(nc.psum_tensor("pw%d" % j, [128, 1024], F32)) for j in range(4)]

    def bank(b):
        return PW[b // 2][:, (b % 2) * 512:(b % 2) * 512 + 512]

    def bk(b):
        return bank(b)[:, 0:TT]

    def pw3(j):
        return PW[j][:, :].rearrange("p (h i) -> p h i", h=8)

    def pwb(j):
        return PW[j][:, 0:512].bitcast(BF16).rearrange("p (h i) -> p h i", h=8)

    sems = {k: es.enter_context(nc.semaphore("s_" + k)) for k in ("pe", "act", "dve", "pool")}
    dnames = ["setup", "c_in", "c_o", "c_pw2", "c_out", "c_gu", "c_dn"] + \
        ["w%d" % i for i in range(NSLOT)] + ["xl%d" % s for s in range(NSUB)] + ["st%d" % s for s in range(NSUB)]
    dsems = {k: es.enter_context(nc.semaphore("d_" + k)) for k in dnames}

    def MM(o, l, r, start, stop, rd, wr):
        P.add("pe", lambda e: e.matmul(out=o, lhsT=l, rhs=r, start=start, stop=stop), rd, wr)

    def TR(o, i, idn, rd, wr):
        P.add("pe", lambda e: e.transpose(out=o, in_=i, identity=idn), rd, wr)

    def ACT(o, i, func, rd, wr, bias=None, scale=None, accum=None):
        kw = {}
        if bias is not None:
            kw["bias"] = bias
        if scale is not None:
            kw["scale"] = scale
        if accum is not None:
            kw["accum_out"] = accum
        P.add("act", lambda e: e.activation(out=o, in_=i, func=func, **kw), rd, wr)

    def TS(eng, o, i, s1, s2, op0, op1, rd, wr):
        if op1 is None:
            P.add(eng, lambda e: e.tensor_scalar(out=o, in0=i, scalar1=s1, scalar2=None, op0=op0), rd, wr)
        else:
            P.add(eng, lambda e: e.tensor_scalar(out=o, in0=i, scalar1=s1, scalar2=s2, op0=op0, op1=op1), rd, wr)

    def TTo(eng, o, a, b, op, rd, wr):
        P.add(eng, lambda e: e.tensor_tensor(out=o, in0=a, in1=b, op=op), rd, wr)

    def STT(o, a, sc, b, op0, op1, rd, wr):
        P.add("dve", lambda e: e.scalar_tensor_tensor(out=o, in0=a, scalar=sc, in1=b, op0=op0, op1=op1), rd, wr)

    def CP(eng, o, i, rd, wr):
        if eng == "act":
            P.add("act", lambda e: e.activation(out=o, in_=i, func=AF.Copy), rd, wr)
        else:
            P.add(eng, lambda e: e.tensor_copy(out=o, in_=i), rd, wr)

    def MSET(eng, o, v, wr):
        P.add(eng, lambda e: e.memset(o, v), (), wr)

    def DMA(eng, o, i, rd, wr, sem, serial=True, slow=False):
        if slow:
            P.add(eng, lambda e: e.dma_start(out=o, in_=i, allow_slow_non_contiguous=True), rd, wr, dma=sem, serial=serial)
        else:
            P.add(eng, lambda e: e.dma_start(out=o, in_=i), rd, wr, dma=sem, serial=serial)

    def cast(src, dst, nrows, sem, key, rstep=128):
        if "cast" in SKIP:
            return []
        for r0 in range(0, nrows, rstep):
            r1 = min(nrows, r0 + rstep)
            DMA("pool", dst[r0:r1, :], src[r0:r1, :], (), [(key, r0)], sem, serial=False)
        return [(key, r0) for r0 in range(0, nrows, rstep)]

    K_in = cast(w_in, wb_in, D, "c_in", "wb_in")
    for s in range(NSUB):
        DMA("sp", XIN[s][:, :], x[s * 128:(s + 1) * 128, :], (), [("xin", s)], "xl%d" % s)
    MSET("dve", rows[:, :], 0.0, [("wr", 3)])
    prow = [(norm1_w, R_N1, 1), (b_glu, R_BGLU, 2), (b_gates, R_BGAT, 2), (cm_dw_b, R_DWB, 1), (cm_ln_w, R_LNW, 1),
            (cm_ln_b, R_LNB, 1), (cm_b_pw2, R_BPW2, 1), (norm2_w, R_N2, 1)]
    for ap_, r0, n in prow:
        DMA("sp", rows[r0:r0 + n, :], ap_.rearrange("(r c) -> r c", r=n), (), [("wr", 3)], "setup", serial=False)
    DMA("sp", rows[R_DW:R_DW + 31, :], cm_dw_w[:, :], (), [("wr", 3)], "setup", serial=False)
    for k in range(4):
        DMA("sp", rows[R_DNC + 3 * k:R_DNC + 3 * k + 3, :], dn_conv_w[k, :].rearrange("(r c) -> r c", r=3), (), [("wr", 3)],
            "setup", serial=False)
    DMA("sp", dnw[:, :], dn_norm_w.rearrange("(p o) -> p o", o=1), (), ["dnw"], "setup", serial=False)
    DMA("sp", dtb[:, :], dn_dt_bias.partition_broadcast(128), (), ["dtb"], "setup", serial=False)
    DMA("sp", nega[:, :], dn_A_log.partition_broadcast(128), (), ["nega"], "setup", serial=False)
    DMA("sp", nfw[:, :], norm_f_w.partition_broadcast(128), (), ["nfw"], "setup", serial=False)
    K_o = cast(dn_w_o, wb_o, D, "c_o", "wb_o", 256)
    K_pw2 = cast(cm_w_pw2, wb_pw2, D, "c_pw2", "wb_pw2", 256)
    K_out = cast(w_out, wb_out, D, "c_out", "wb_out", 256)
    K_gu = cast(ffn_gu, wb_gu, D, "c_gu", "wb_gu")
    K_dn = cast(ffn_dn, wb_dn, FH, "c_dn", "wb_dn", 256)
    DMA("sp", wbd[:, :, :], wb_in[:, 4096:4112].rearrange("(k p) c -> p k c", p=128), K_in, ["wbd"], "setup",
        serial=False, slow=True)

    MSET("pool", ones_f[:, :], 1.0, ["ones_f"])
    MSET("pool", ones_b[:, :], 1.0, ["ones_b"])
    MSET("pool", idf[:, :], 0.0, ["idf"])
    P.add("pool", lambda e: e.affine_select(out=idf[:, :], in_=idf[:, :], pattern=[[-1, 128]], compare_op=ALU.not_equal,
                                            fill=1.0, base=0, channel_multiplier=1), ["idf"], ["idf"])
    CP("pool", ident_b[:, :], idf[:, :], ["idf"], ["ident_b"])
    P.add("pool", lambda e: e.affine_select(out=tri_f[:, :], in_=ones_f[:, :], pattern=[[1, 128]], compare_op=ALU.is_ge,
                                            fill=0.0, base=0, channel_multiplier=-1), ["ones_f"], ["tri_f"])
    MSET("pool", tri_f[0:64, 64:128], 0.0, ["tri_f"])
    CP("pool", chk_f[:, :], ones_f[:, :], ["ones_f"], ["chk_f"])
    MSET("pool", chk_f[0:64, 64:128], 0.0, ["chk_f"])
    MSET("pool", chk_f[64:128, 0:64], 0.0, ["chk_f"])
    TS("pool", neg_t[:, :], tri_f[:, :], -1.0, BIG, ALU.add, ALU.mult, ["tri_f"], ["neg_t"])
    P.add("pool", lambda e: e.affine_select(out=pos_s[:, :], in_=ones_f[:, :], pattern=[[-1, 128]], compare_op=ALU.is_gt,
                                            fill=0.0, base=0, channel_multiplier=1), ["ones_f"], ["pos_s"])
    MSET("pool", pos_s[64:128, 0:64], 0.0, ["pos_s"])
    TS("pool", pos_s[:, :], pos_s[:, :], -BIG, BIG, ALU.mult, ALU.add, ["pos_s"], ["pos_s"])
    MSET("pool", HALO[:, :, :], 0.0, ["halo"])
    MSET("pool", CIN[:, :, 0:30], 0.0, [("cin", c) for c in range(8)])
    MSET("pool", S[:, :, :], 0.0, [("S", h) for h in range(NH)])
    MSET("pool", SBF[:, :, :], 0.0, [("SBF", h) for h in range(NH)])
    for c in range(8):
        TR(bank(0)[:, c * 64:c * 64 + NROWS], rows[0:NROWS, c * 128:(c + 1) * 128], idf[0:NROWS, 0:NROWS],
           [("wr", 3), "idf"], [("ps", 0)])
    CP("dve", par[:, :, :], bank(0).rearrange("p (c r) -> p c r", c=8)[:, :, 0:NROWS], [("ps", 0)], ["par"])
    ACT(nega[:, :], nega[:, :], AF.Exp, ["nega"], ["nega"])
    TS("dve", nega[:, :], nega[:, :], -1.0, None, ALU.mult, None, ["nega"], ["nega"])

    loads = []

    def plan_tile():
        L = []
        for g in range(6):
            L.append((wb_in, 0, 8, g * 512, 512, K_in))
        for g in range(2):
            L.append((wb_in, 0, 8, 4112 + g * 512, 512, K_in))
            L.append((wb_in, 0, 8, 4112 + 1024 + g * 512, 512, K_in))
        for g in range(2):
            L.append((wb_in, 0, 8, 3072 + g * 512, 512, K_in))
        for g in range(4):
            L.append((wb_in, 0, 8, 6160 + g * 512, 512, K_in))
        for g in range(2):
            L.append((wb_o, 0, 8, g * 512, 512, K_o))
        for g in range(2):
            L.append((wb_pw2, 0, 8, g * 512, 512, K_pw2))
        for g in range(2):
            L.append((wb_out, 0, 8, g * 512, 512, K_out))
        for g in range(6):
            nc_ = 512 if g < 5 else 256
            L.append((wb_gu, 0, 8, g * 512, nc_, K_gu))
            L.append((wb_gu, 0, 8, FH + g * 512, nc_, K_gu))
        for hf in range(2):
            for kg in range(3):
                nk = 8 if kg < 2 else 6
                L.append((wb_dn, kg * 1024, nk, hf * 512, 512, K_dn))
        return L

    for t in range(NT):
        loads.extend(plan_tile())
    issued = [0]
    consumed = [0]

    def issue_to(n):
        while issued[0] < min(n, len(loads)):
            m = issued[0]
            dr, r0, nk, c0, ncol, ck = loads[m]
            slot = m % NSLOT
            src = dr[r0:r0 + 128 * nk, c0:c0 + ncol].rearrange("(k p) c -> p k c", p=128)
            DMA("sp", WR[slot][:, 0:nk, 0:ncol], src, ck, [("wr", slot)], "w%d" % slot)
            issued[0] += 1

    def next_w():
        n = consumed[0]
        consumed[0] += 1
        assert issued[0] >= n + 1 or n < NSLOT + 100000
        issue_to(n + 1)
        return n % NSLOT

    def fin():
        issue_to(consumed[0] + NSLOT)

    dgc = [0]
    DG_ENG = ("dve", "pool")

    def build_diag(sc):
        i = dgc[0] % NDG
        eng = DG_ENG[dgc[0] % len(DG_ENG)]
        dgc[0] += 1
        if eng == "act":
            ACT(DG[i][:, :], ident_b[:, :], AF.Identity, ["ident_b", "par"], [("dg", i)], scale=sc)
        else:
            TS(eng, DG[i][:, :], ident_b[:, :], sc, 0.0, ALU.mult, ALU.add, ["ident_b", "par"], [("dg", i)])
        return DG[i][:, :], ("dg", i)

    def tok_rstd(src, junk, jkey, col, rd, scale_n):
        ACT(junk, src, AF.Square, rd, [jkey, ("st", col)], accum=STT_[:, col:col + 1])
        TS("dve", STT_[:, col:col + 1], STT_[:, col:col + 1], scale_n, EPS, ALU.mult, ALU.add, [("st", col)], [("st", col)])
        ACT(STT_[:, col:col + 1], STT_[:, col:col + 1], AF.Ln, [("st", col)], [("st", col)])
        ACT(STT_[:, col:col + 1], STT_[:, col:col + 1], AF.Exp, [("st", col)], [("st", col)], scale=-0.5)

    def norm_stats(s, src, skey):
        tok_rstd(src[:, :], HB[s % 2][:, :], ("hb", s % 2), s, [skey], 1.0 / D)

    def norm_apply(s, src, skey, nrow, pb0):
        hb = HB[s % 2]
        TS("dve", hb[:, :], src[:, :], STT_[:, s:s + 1], None, ALU.mult, None, [skey, ("st", s)], [("hb", s % 2)])
        for c in range(8):
            TR(bank(pb0 + c // 4).bitcast(BF16)[:, (c % 4) * 128:(c % 4) * 128 + 128],
               hb[:, c * 128:(c + 1) * 128], ident_b[:, :], [("hb", s % 2), "ident_b"], [("ps", pb0 + c // 4)])
        for hf in range(2):
            srcp = bank(pb0 + hf).bitcast(BF16)[:, 0:512].rearrange("p (c i) -> p c i", c=4)
            TTo("dve", HT[:, hf * 4:hf * 4 + 4, s * 128:(s + 1) * 128], srcp,
                par[:, hf * 4:hf * 4 + 4, nrow:nrow + 1].to_broadcast([128, 4, 128]), ALU.mult,
                [("ps", pb0 + hf), "par"], [("ht", s)])

    HTK = [("ht", s) for s in range(NSUB)]
    subs = lambda s: slice(s * 128, (s + 1) * 128)

    try:
      stage(0)
      for t in range(NT):
        tok0 = t * TT
        if t == 0:
            for s in range(NSUB):
                norm_stats(s, XIN[s], ("xin", s))
            for s in range(NSUB):
                norm_apply(s, XIN[s], ("xin", s), R_N1, 0)

        stage(1)
        for s in range(NSUB):
            for kc in range(8):
                MM(bank(2)[:, s * 16:s * 16 + 16], HT[:, kc, subs(s)], wbd[:, kc, :], kc == 0, kc == 7,
                   [("ht", s), "wbd"], [("ps", 2)])
        bdp = bank(2)[:, 0:NSUB * 16].rearrange("p (s c) -> p s c", s=NSUB)
        ACT(SM[:, :, 0, :], bdp[:, :, 0:8], AF.Sigmoid, [("ps", 2)], ["sm"])
        TTo("dve", SM[:, :, 1, :], bdp[:, :, 8:16], dtb[:, :].unsqueeze(1).to_broadcast([128, NSUB, NH]), ALU.add,
            [("ps", 2), "dtb"], ["sm"])


        stage(2)
        def qkv_post(ch):
            seg, hh = ch // 8, ch % 8
            pbk = ch % 2
            pre = PRE[ch % 2]
            CP("pool", pre[:, 0:4], HALO[:, ch, :], ["halo"], [("pre", ch % 2)])
            ACT(pre[:, 4:4 + TT], bk(pbk), AF.Identity, [("ps", pbk)], [("pre", ch % 2)])
            CP("pool", HALO[:, ch, :], pre[:, TT:TT + 4], [("pre", ch % 2)], ["halo"])
            for k in range(4):
                dg, dk_ = build_diag(par[:, ch % 8, R_DNC + 3 * k + seg:R_DNC + 3 * k + seg + 1])
                MM(bk(2 + pbk), dg, pre[:, 1 + k:1 + k + TT], k == 0, k == 3, [dk_, ("pre", ch % 2)], [("ps", 2 + pbk)])
            dst = (QN, KN, VS)[seg]
            ACT(dst[:, hh, :], bk(2 + pbk), AF.Silu, [("ps", 2 + pbk)], [("qkv", seg, hh)])

        for g in range(6):
            slot = next_w()
            for j in range(4):
                ch = g * 4 + j
                pbk = ch % 2
                for kc in range(8):
                    MM(bk(pbk), WR[slot][:, kc, j * 128:(j + 1) * 128], HT[:, kc, :], kc == 0, kc == 7,
                       HTK + [("wr", slot)], [("ps", pbk)])
                if ch >= 1:
                    qkv_post(ch - 1)
            fin()
        qkv_post(23)
        stage(3)
        stage(4)
        for g in range(2):
            sa = next_w()
            sbk = next_w()
            for j in range(4):
                c = g * 4 + j
                ba, bb = 2 * (c % 2), 2 * (c % 2) + 1
                for kc in range(8):
                    MM(bk(ba), WR[sa][:, kc, j * 128:(j + 1) * 128], HT[:, kc, :], kc == 0, kc == 7,
                       HTK + [("wr", sa)], [("ps", ba)])
                for kc in range(8):
                    MM(bk(bb), WR[sbk][:, kc, j * 128:(j + 1) * 128], HT[:, kc, :], kc == 0, kc == 7,
                       HTK + [("wr", sbk)], [("ps", bb)])
                tw = TMPW[c % 2]
                ACT(tw[:, :], bk(bb), AF.Sigmoid, [("ps", bb), "par"], [("tmpw", c % 2)], bias=par[:, c, R_BGLU + 1:R_BGLU + 2])
                STT(CIN[:, c, 30:30 + TT], bk(ba), par[:, c, R_BGLU:R_BGLU + 1], tw[:, :], ALU.add, ALU.mult,
                    [("ps", ba), "par", ("tmpw", c % 2)], [("cin", c)])
            fin()
        stage(7)
        ACT(SM[:, :, 2, :], SM[:, :, 1, :], AF.Exp, ["sm"], ["sm"])
        ACT(SM[:, :, 3, :], SM[:, :, 2, :], AF.Ln, ["sm"], ["sm"], bias=1.0)
        TTo("dve", SM[:, :, 4, :], SM[:, :, 3, :], nega[:, :].unsqueeze(1).to_broadcast([128, NSUB, NH]), ALU.mult,
            ["sm", "nega"], ["sm"])
        for s in range(NSUB):
            MM(bank(2)[:, 64 + s * 16:64 + s * 16 + 8], tri_f[:, :], SM[:, s, 4, :], True, True, ["tri_f", "sm"], [("ps", 2)])
            MM(bank(2)[:, 64 + s * 16 + 8:64 + s * 16 + 16], chk_f[:, :], SM[:, s, 4, :], True, True, ["chk_f", "sm"], [("ps", 2)])
        gcp = bank(2)[:, 64:64 + NSUB * 16].rearrange("p (s c) -> p s c", s=NSUB)
        CP("dve", SM[:, :, 5, :], gcp[:, :, 0:8], [("ps", 2)], ["sm"])
        CP("dve", SM[:, :, 6, :], gcp[:, :, 8:16], [("ps", 2)], ["sm"])
        ACT(SM[:, :, 7, :], SM[:, :, 5, :], AF.Exp, ["sm"], ["sm"])
        TTo("dve", SM[:, :, 8, :], SM[:, :, 0, :], SM[:, :, 7, :], ALU.mult, ["sm"], ["sm"])
        TTo("dve", SM[:, :, 9, :], SM[:, :, 6, :], SM[:, :, 5, :], ALU.subtract, ["sm"], ["sm"])
        ACT(SM[:, :, 10, :], SM[:, :, 9, :], AF.Exp, ["sm"], ["sm"])

        stage(8)
        def l2_sq(seg, buf, hh, i2):
            TTo("dve", SQW[i2][:, :], buf[:, hh, :], buf[:, hh, :], ALU.mult, [("qkv", seg, hh)], [("sqw", i2)])

        def l2_mm(i2):
            MM(bk(4 + i2), ones_b[:, :], SQW[i2][:, :], True, True, [("sqw", i2), "ones_b"], [("ps", 4 + i2)])
            ACT(TMPW[i2][:, :], bk(4 + i2), AF.Ln, [("ps", 4 + i2)], [("tmpw", i2)], bias=EPS)
            ACT(TMPW[i2][:, :], TMPW[i2][:, :], AF.Exp, [("tmpw", i2)], [("tmpw", i2)], scale=-0.5)

        def l2_mul(seg, buf, hh, i2):
            TTo("dve", buf[:, hh, :], buf[:, hh, :], TMPW[i2][:, :], ALU.mult, [("qkv", seg, hh), ("tmpw", i2)],
                [("qkv", seg, hh)])

        def ln_cast(c):
            ACT(SQL[0][:, :], CCV[:, c, :], AF.Copy, [*rk_ccv(c)], [("sql", 0)])
            ACT(SQL[1][:, :], CCV[:, c, :], AF.Square, [*rk_ccv(c)], [("sql", 1)])

        def ln_mm(c, which):
            MM(bk(6 + which), ones_b[:, :], SQL[which][:, :], c == 0, c == 7, [("sql", which), "ones_b"], [("ps", 6 + which)])

        for c in range(8):
            pbk = c % 2
            for k in range(31):
                dg, dk_ = build_diag(par[:, c, R_DW + k:R_DW + k + 1])
                MM(bk(pbk), dg, CIN[:, c, k:k + TT], k == 0, k == 30, [dk_, ("cin", c)], [("ps", pbk)])
                if k == 2:
                    l2_sq(0, QN, c, 0)
                elif k == 4:
                    l2_sq(1, KN, c, 1)
                elif k == 6 and c >= 1:
                    ln_cast(c - 1)
                elif k == 10:
                    l2_mm(0)
                elif k == 14:
                    l2_mm(1)
                elif k == 17 and c >= 1:
                    ln_mm(c - 1, 0)
                elif k == 20 and c >= 1:
                    ln_mm(c - 1, 1)
                elif k == 24:
                    l2_mul(0, QN, c, 0)
                elif k == 28:
                    l2_mul(1, KN, c, 1)
            ACT(CCV[:, c, :], bk(pbk), AF.Identity, [("ps", pbk), "par"], [*rk_ccv(c)], bias=par[:, c, R_DWB:R_DWB + 1])
            CP("pool", CIN[:, c, 0:30], CIN[:, c, TT:TT + 30], [("cin", c)], [("cin", c)])
        ln_cast(7)
        ln_mm(7, 0)
        ln_mm(7, 1)
        stage(9)
        TS("dve", MA[0][:, :], bk(6), 1.0 / D, None, ALU.mult, None, [("ps", 6)], [("ma", 0)])
        TTo("dve", TMPW[0][:, :], MA[0][:, :], MA[0][:, :], ALU.mult, [("ma", 0)], [("tmpw", 0)])
        STT(MA[1][:, :], bk(7), 1.0 / D, TMPW[0][:, :], ALU.mult, ALU.subtract, [("ps", 7), ("tmpw", 0)], [("ma", 1)])
        ACT(MA[1][:, :], MA[1][:, :], AF.Ln, [("ma", 1)], [("ma", 1)], bias=EPS)
        ACT(MA[1][:, :], MA[1][:, :], AF.Exp, [("ma", 1)], [("ma", 1)], scale=-0.5)
        for g in range(2):
            slot = next_w()
            for j in range(4):
                c = g * 4 + j
                pbk = j % 2
                for kc in range(8):
                    MM(bk(pbk), WR[slot][:, kc, j * 128:(j + 1) * 128], HT[:, kc, :], kc == 0, kc == 7,
                       HTK + [("wr", slot)], [("ps", pbk)])
                ACT(ZS[:, c, :], bk(pbk), AF.Silu, [("ps", pbk)], [("zs", c)])
                TTo("dve", CCV[:, c, :], CCV[:, c, :], MA[0][:, :], ALU.subtract, [*rk_ccv(c), ("ma", 0)], [*rk_ccv(c)])
                TTo("pool" if c % 3 == 2 else "dve", CCV[:, c, :], CCV[:, c, :], MA[1][:, :], ALU.mult, [*rk_ccv(c), ("ma", 1)],
                    [*rk_ccv(c)])
                ACT(CN[:, c, :], CCV[:, c, :], AF.Silu, [*rk_ccv(c), "par"], [("cin", c)],
                    bias=par[:, c, R_LNB:R_LNB + 1], scale=par[:, c, R_LNW:R_LNW + 1])
            fin()

        stage(10)
        HS = (0, 1)

        def hsl(hs):
            return slice(4 * hs, 4 * hs + 4)

        def hrange(hs):
            return range(4 * hs, 4 * hs + 4)

        def pwbh(j, hs):
            return bank(2 * j + hs).bitcast(BF16)[:, 0:512].rearrange("p (h i) -> p h i", h=4)

        def bc4(v, hs):
            return v[:, hsl(hs)].unsqueeze(2).to_broadcast([128, 4, 128])

        def mb4(m):
            return m[:, :].unsqueeze(1).to_broadcast([128, 4, 128])

        def d_prep(s):
            g_, beta_, gc_ = SM[:, s, 4, :], SM[:, s, 0, :], SM[:, s, 5, :]
            for hs in HS:
                TTo("dve", TRIG[:, hsl(hs), :], mb4(tri_f), bc4(g_, hs), ALU.mult, ["tri_f", "sm"], [("trig", hs)])
            for hs in HS:
                for h in hrange(hs):
                    MM(pw3(1)[:, h, :], ones_f[:, :], TRIG[:, h, :], True, True, [("trig", hs), "ones_f"], [("ps", 2 + hs)])
            for hs in HS:
                ACT(EGB[:, hsl(hs), :], pw3(1)[:, hsl(hs), :], AF.Exp, [("ps", 2 + hs)], [("egb", hs)])
                TTo("dve", XX[:, hsl(hs), :], pw3(1)[:, hsl(hs), :], bc4(gc_, hs), ALU.subtract, [("ps", 2 + hs), "sm"], [("xx", hs)])
            for hs in HS:
                TTo("dve", XB[:, hsl(hs), :], XX[:, hsl(hs), :], mb4(pos_s), ALU.add, [("xx", hs), "pos_s"], [("xb", hs)])
            for hs in HS:
                ACT(XB[:, hsl(hs), :], XB[:, hsl(hs), :], AF.Exp, [("xb", hs)], [("xb", hs)], scale=-1.0)
            for hs in HS:
                TTo("dve", XB[:, hsl(hs), :], XB[:, hsl(hs), :], bc4(beta_, hs), ALU.mult, [("xb", hs), "sm"], [("xb", hs)])

        def d_head(s):
            sl = subs(s)
            stage(11)
            for hs in HS:
                for h in hrange(hs):
                    MM(pw3(2)[:, h, :], KN[:, h, sl], KN[:, h, sl], True, True, [("qkv", 1, h)], [("ps", 4 + hs)])
            for hs in HS:
                TTo("dve", MMc[0][:, hsl(hs), :], pw3(2)[:, hsl(hs), :], XB[:, hsl(hs), :], ALU.mult, [("ps", 4 + hs), ("xb", hs)],
                    [("mm", 0, hs)])
            for hs in HS:
                for h in hrange(hs):
                    TR(pwbh(3, hs)[:, h - 4 * hs, :], MMc[0][:, h, :], ident_b[:, :], [("mm", 0, hs), "ident_b"], [("ps", 6 + hs)])
            for hs in HS:
                CP("act", NNc[0][:, hsl(hs), :], pwbh(3, hs), [("ps", 6 + hs)], [("nn", 0, hs)])
                STT(PBF[:, hsl(hs), :], pwbh(3, hs), -1.0, mb4(idf), ALU.mult, ALU.add, [("ps", 6 + hs), "idf"], [("pbf", hs)])

        def d_tail(s):
            sl = subs(s)
            beta_ = SM[:, s, 0, :]
            stage(12)
            for hs in HS:
                TTo("pool", XA[:, hsl(hs), :], XX[:, hsl(hs), :], mb4(neg_t), ALU.add, [("xx", hs), "neg_t"], [("xa", hs)])
                ACT(XA[:, hsl(hs), :], XA[:, hsl(hs), :], AF.Exp, [("xa", hs)], [("xa", hs)])
            for l in range(1, 6):
                cur, nxt = (l - 1) % 2, l % 2
                if l == 2:
                    for hs in HS:
                        STT(QD[:, hsl(hs), :], QN[:, hsl(hs), sl], 128.0 ** -0.5, EGB[:, hsl(hs), :], ALU.mult, ALU.mult,
                            [("qkv", 0, h) for h in hrange(hs)] + [("egb", hs)], [("qd", hs)])
                if l == 5:
                    for hs in HS:
                        for h in hrange(hs):
                            TR(pwbh(2, hs)[:, h - 4 * hs, :], VS[:, h, sl], ident_b[:, :], [("qkv", 2, h), "ident_b"], [("ps", 4 + hs)])
                    for hs in HS:
                        TTo("dve", VB[:, hsl(hs), :], pwbh(2, hs), bc4(beta_, hs), ALU.mult, [("ps", 4 + hs), "sm"], [("vb", hs)])
                for hs in HS:
                    for h in hrange(hs):
                        MM(pw3(1)[:, h, :], NNc[cur][:, h, :], MMc[cur][:, h, :], True, True, [("nn", cur, hs), ("mm", cur, hs)],
                           [("ps", 2 + hs)])
                    if l < 5:
                        for h in hrange(hs):
                            MM(pw3(2)[:, h, :], MMc[cur][:, h, :], NNc[cur][:, h, :], True, True, [("nn", cur, hs), ("mm", cur, hs)],
                               [("ps", 4 + hs)])
                for hs in HS:
                    CP("act", MMc[nxt][:, hsl(hs), :], pw3(1)[:, hsl(hs), :], [("ps", 2 + hs)], [("mm", nxt, hs)])
                    if l < 5:
                        CP("act" if hs == 0 else "dve", NNc[nxt][:, hsl(hs), :], pw3(2)[:, hsl(hs), :], [("ps", 4 + hs)], [("nn", nxt, hs)])
                for hs in HS:
                    for h in hrange(hs):
                        MM(pw3(3)[:, h, :], MMc[nxt][:, h, :], PBF[:, h, :], True, False, [("mm", nxt, hs), ("pbf", hs)], [("ps", 6 + hs)])
                        MM(pw3(3)[:, h, :], ident_b[:, :], PBF[:, h, :], False, True, ["ident_b", ("pbf", hs)], [("ps", 6 + hs)])
                for hs in HS:
                    CP("dve", PBF[:, hsl(hs), :], pw3(3)[:, hsl(hs), :], [("ps", 6 + hs)], [("pbf", hs)])
                if l == 4 and pending_onorm[0] is not None:
                    d_onorm(pending_onorm[0])
                    pending_onorm[0] = None
            stage(13)
            for hs in HS:
                for h in hrange(hs):
                    TR(pwbh(1, hs)[:, h - 4 * hs, :], KN[:, h, sl], ident_b[:, :], [("qkv", 1, h), "ident_b"], [("ps", 2 + hs)])
            for hs in HS:
                TTo("dve", KBG[:, hsl(hs), :], pwbh(1, hs), bc4(SM[:, s, 8, :], hs), ALU.mult, [("ps", 2 + hs), "sm"], [("pf", hs)])
                TTo("dve", KD[:, hsl(hs), :], pwbh(1, hs), bc4(SM[:, s, 10, :], hs), ALU.mult, [("ps", 2 + hs), "sm"], [("pf", hs)])
            stage(14)
            for hs in HS:
                for h in hrange(hs):
                    MM(pw3(3)[:, h, :], KBG[:, h, :], PBF[:, h, :], True, True, [("pf", hs), ("pbf", hs)], [("ps", 6 + hs)])
            for hs in HS:
                CP("act", WT[:, hsl(hs), :], pw3(3)[:, hsl(hs), :], [("ps", 6 + hs)], [("wt", hs)])
            for hs in HS:
                for h in hrange(hs):
                    MM(pw3(1)[:, h, :], PBF[:, h, :], VB[:, h, :], True, True, [("vb", hs), ("pbf", hs)], [("ps", 2 + hs)])
            for hs in HS:
                CP("act", U[:, hsl(hs), :], pw3(1)[:, hsl(hs), :], [("ps", 2 + hs)], [("xb", hs)])
            stage(15)
            for hs in HS:
                for h in hrange(hs):
                    MM(pw3(2)[:, h, :], KN[:, h, sl], QN[:, h, sl], True, True, [("qkv", 1, h), ("qkv", 0, h)], [("ps", 4 + hs)])
            for hs in HS:
                STT(ATT[:, hsl(hs), :], pw3(2)[:, hsl(hs), :], 128.0 ** -0.5, XA[:, hsl(hs), :], ALU.mult, ALU.mult,
                    [("ps", 4 + hs), ("xa", hs)], [("att", hs)])
            stage(16)
            for chn in range(2):
                r0 = 64 * chn
                rs = slice(r0, r0 + 64)
                for h in range(NH):
                    MM(pw3(3)[:, h, :], WT[:, h, :], SBF[:, h, :], True, True, [("wt", h // 4), ("SBF", h)], [("ps", 6 + h // 4)])
                for hs in HS:
                    TTo("dve", VN[rs, hsl(hs), :], U[rs, hsl(hs), :], pw3(3)[rs, hsl(hs), :], ALU.subtract, [("xb", hs), ("ps", 6 + hs)],
                        [("vn", chn, hs)])
                for h in range(NH):
                    MM(pw3(1)[:, h, :], KD[rs, h, :], VN[rs, h, :], True, True, [("pf", h // 4), ("vn", chn, h // 4)], [("ps", 2 + h // 4)])
                for h in range(NH):
                    MM(pw3(0)[:, h, rs], SBF[:, h, :], QD[:, h, rs], True, False, [("qd", h // 4), ("SBF", h)], [("ps", h // 4)])
                    MM(pw3(0)[:, h, rs], VN[rs, h, :], ATT[rs, h, rs], False, True, [("vn", chn, h // 4), ("att", h // 4)], [("ps", h // 4)])
                for h in range(NH):
                    STT(S[:, h, :], S[:, h, :], EGB[:, h, r0 + 63:r0 + 64], pw3(1)[:, h, :], ALU.mult, ALU.add,
                        [("S", h), ("egb", h // 4), ("ps", 2 + h // 4)], [("S", h)])
                    CP("act" if h % 2 == 0 else "pool", SBF[:, h, :], S[:, h, :], [("S", h)], [("SBF", h)])

        def d_onorm(s):
            sl = subs(s)
            stage(17)
            TK = [("trig", 0), ("trig", 1)]
            WK = [("wt", 0), ("wt", 1)]
            ACT(SQ[:, :, :], pw3(0), AF.Square, [("ps", 0), ("ps", 1)], TK)
            for h in range(NH):
                MM(pw3(2)[:, h, :], ones_b[:, :], SQ[:, h, :], True, True, TK + ["ones_b"], [("ps", 4 + h // 4)])
            ACT(RR[:, :, :], pw3(2), AF.Ln, [("ps", 4), ("ps", 5)], TK, bias=EPS, scale=1.0 / 128)
            ACT(RR[:, :, :], RR[:, :, :], AF.Exp, TK, TK, scale=-0.5)
            STT(T1[:, :, :], pw3(0), dnw[:, 0:1], RR[:, :, :], ALU.mult, ALU.mult, [("ps", 0), ("ps", 1), "dnw"] + TK, WK)
            TTo("dve", OG[:, :, sl], T1[:, :, :], ZS[:, :, sl], ALU.mult, WK + [("zs", c) for c in range(8)], [("og", s)])

        pending_onorm = [None]
        d_prep(0)
        d_head(0)
        for s in range(NSUB):
            d_tail(s)
            if s + 1 < NSUB:
                d_prep(s + 1)
                d_head(s + 1)
                pending_onorm[0] = s
            else:
                d_onorm(s)

        stage(5)
        for g in range(4):
            slot = next_w()
            for j in range(4):
                c = g * 4 + j
                pbk = j % 2
                for kc in range(8):
                    MM(bk(pbk), WR[slot][:, kc, j * 128:(j + 1) * 128], HT[:, kc, :], kc == 0, kc == 7,
                       HTK + [("wr", slot)], [("ps", pbk)])
                ACT(GATES[:, c, :], bk(pbk), AF.Sigmoid, [("ps", pbk), "par"], [*rk_gates(c)],
                    bias=par[:, c % 8, R_BGAT + c // 8:R_BGAT + c // 8 + 1])
            fin()

        stage(21)
        OGK = [("og", s) for s in range(NSUB)]
        for g in range(2):
            slot = next_w()
            for j in range(4):
                c = g * 4 + j
                pbk = j % 2
                for kc in range(8):
                    MM(bk(pbk), WR[slot][:, kc, j * 128:(j + 1) * 128], OG[:, kc, :], kc == 0, kc == 7,
                       OGK + [("wr", slot)], [("ps", pbk)])
                TTo("dve", MRG[:, c, :], bk(pbk), GATES[:, c, :], ALU.mult, [("ps", pbk), *rk_gates(c)], [("zs", c)])
            fin()
        for g in range(2):
            slot = next_w()
            for j in range(4):
                c = g * 4 + j
                pbk = j % 2
                for kc in range(8):
                    MM(bk(pbk), WR[slot][:, kc, j * 128:(j + 1) * 128], CN[:, kc, :], kc == 0, kc == 7,
                       [("cin", k) for k in range(8)] + [("wr", slot)], [("ps", pbk)])
                STT(MA[pbk][:, :], bk(pbk), par[:, c, R_BPW2:R_BPW2 + 1], GATES[:, 8 + c, :], ALU.add, ALU.mult,
                    [("ps", pbk), "par", *rk_gates(8 + c)], [("ma", pbk)])
                TTo("pool", MRG[:, c, :], MRG[:, c, :], MA[pbk][:, :], ALU.add, [("zs", c), ("ma", pbk)], [("zs", c)])
            fin()
        stage(22)
        MK = [("zs", c) for c in range(8)]
        wslots = [next_w(), next_w()]
        for s in range(NSUB):
            for hf in range(2):
                pbk = (2 * s + hf) % 4
                for kc in range(8):
                    MM(bank(pbk), MRG[:, kc, subs(s)], WR[wslots[hf]][:, kc, :], kc == 0, kc == 7,
                       MK + [("wr", wslots[hf])], [("ps", pbk)])
                TTo("dve", XS[s][:, hf * 512:(hf + 1) * 512], XIN[s][:, hf * 512:(hf + 1) * 512], bank(pbk), ALU.add,
                    [("xin", s), ("ps", pbk)], [("xs", s)])
            norm_stats(s, XS[s], ("xs", s))
            if s >= 1:
                norm_apply(s - 1, XS[s - 1], ("xs", s - 1), R_N2, 4)
        fin()
        if t + 1 < NT:
            for s in range(NSUB):
                DMA("sp", XIN[s][:, :], x[tok0 + TT + s * 128:tok0 + TT + (s + 1) * 128, :], (), [("xin", s)], "xl%d" % s)
        stage(23)
        norm_apply(NSUB - 1, XS[NSUB - 1], ("xs", NSUB - 1), R_N2, 4)
        stage(24)
        for g in range(6):
            sg = next_w()
            su = next_w()
            nj = 4 if g < 5 else 2
            for j in range(nj):
                fc = g * 4 + j
                bg_, bu_ = 2 * (j % 2), 2 * (j % 2) + 1
                for kc in range(8):
                    MM(bk(bg_), WR[sg][:, kc, j * 128:(j + 1) * 128], HT[:, kc, :], kc == 0, kc == 7, HTK + [("wr", sg)], [("ps", bg_)])
                for kc in range(8):
                    MM(bk(bu_), WR[su][:, kc, j * 128:(j + 1) * 128], HT[:, kc, :], kc == 0, kc == 7, HTK + [("wr", su)], [("ps", bu_)])
                tw = TMPW[fc % 2]
                ACT(tw[:, :], bk(bg_), AF.Silu, [("ps", bg_)], [("tmpw", fc % 2)])
                TTo("dve", FF[:, fc, :], tw[:, :], bk(bu_), ALU.mult, [("tmpw", fc % 2), ("ps", bu_)], [("qkv", fc // 8, fc % 8)])
            fin()
        stage(25)
        if t + 1 < NT:
            for s in range(NSUB):
                norm_stats(s, XIN[s], ("xin", s))
        FK = [("qkv", fc // 8, fc % 8) for fc in range(22)]
        for hf in range(2):
            for kg in range(3):
                slot = next_w()
                nk = 8 if kg < 2 else 6
                for s in range(NSUB):
                    pbk = 4 + s
                    for kk in range(nk):
                        fc = kg * 8 + kk
                        MM(bank(pbk), FF[:, fc, subs(s)], WR[slot][:, kk, :], fc == 0, fc == 21,
                           [FK[fc], ("wr", slot)], [("ps", pbk)])
                    if kg == 0 and hf == 0 and t + 1 < NT:
                        norm_apply(s, XIN[s], ("xin", s), R_N1, 0)
                    if kg == 2:
                        TTo("dve", XS[s][:, hf * 512:(hf + 1) * 512], XS[s][:, hf * 512:(hf + 1) * 512], bank(pbk), ALU.add,
                            [("xs", s), ("ps", pbk)], [("xs", s)])
                        if hf == 1:
                            hb = HB[s % 2]
                            tok_rstd(XS[s][:, :], hb[:, :], ("hb", s % 2), 4 + s, [("xs", s)], 1.0 / D)
                            STT(XS[s][:, :], XS[s][:, :], STT_[:, 4 + s:5 + s], nfw[:, :], ALU.mult, ALU.mult,
                                [("xs", s), ("st", 4 + s), "nfw"], [("xs", s)])
                            DMA("pool", out[tok0 + s * 128:tok0 + (s + 1) * 128, :], XS[s][:, :], [("xs", s)], [("out", s)],
                                "st%d" % s)
                fin()
        stage(26)
    except _Stop:
        pass
    P.add("pool", lambda e: e.engine_nop(), [("out", s) for s in range(NSUB)], ())
    P.emit(sems, dsems)
    es.close()
    return nc


_CACHE = {}


def kernel(**inputs):
    x = np.ascontiguousarray(inputs["x"], dtype=np.float32)
    B, T, _ = x.shape
    if T not in _CACHE:
        _CACHE[T] = build(T)
    nc = _CACHE[T]
    shared = {}
    for k, v in inputs.items():
        if k == "x":
            continue
        a = np.ascontiguousarray(v, dtype=np.float32)
        if k != "norm_f_w":
            a = a[0]
        shared[k] = np.ascontiguousarray(a)
    in_maps = []
    for b in range(B):
        m = dict(shared)
        m["x"] = x[b]
        in_maps.append(m)
    res = run_bass_kernel_spmd(nc, in_maps, core_ids=list(range(B)))
    return np.stack([np.asarray(r["out"], dtype=np.float32) for r in res.results], axis=0)
```

```python
import numpy as np
from contextlib import ExitStack
import concourse.bass as bass
import concourse.mybir as mybir
from concourse.bass_utils import run_bass_kernel_spmd

F32 = mybir.dt.float32
BF16 = mybir.dt.bfloat16
AF = mybir.ActivationFunctionType
ALU = mybir.AluOpType

D = 1024
NH = 8
INC = 8208
FH = 2816
EPS = 1e-6
BIG = 30000.0


class Op:
    __slots__ = ("eng", "fn", "deps", "flag", "val", "dsem", "dval", "idx")


class Prog:
    def __init__(self, nc):
        self.nc = nc
        self.ops = []
        self.last_w = {}
        self.readers = {}
        self.dma_tot = {}
        self.dma_serial = {}

    def add(self, eng, fn, reads=(), writes=(), dma=None, serial=True):
        op = Op()
        op.eng = eng
        op.fn = fn
        op.flag = False
        op.val = None
        op.dsem = dma
        op.dval = None
        op.idx = len(self.ops)
        psr = [k for k in reads if isinstance(k, tuple) and k[0] == "ps"]
        if psr:
            reads = [k for k in reads if not (isinstance(k, tuple) and k[0] == "ps")]
            writes = list(writes) + psr
        deps = {}
        for k in reads:
            w = self.last_w.get(k)
            if w is not None:
                deps[w.idx] = w
        for k in writes:
            w = self.last_w.get(k)
            if w is not None:
                deps[w.idx] = w
            for r in self.readers.get(k, {}).values():
                deps[r.idx] = r
        for k in reads:
            self.readers.setdefault(k, {})[(eng, dma)] = op
        for k in writes:
            self.last_w[k] = op
            self.readers[k] = {}
        out = []
        for d in deps.values():
            if d.dsem is None and d.eng == "pe" and eng == "pe" and dma is None:
                continue
            ov = None
            if d.dsem is not None and not self.dma_serial[d.dsem]:
                ov = self.dma_tot[d.dsem]
            out.append((d, ov))
        op.deps = out
        if dma is not None:
            self.dma_tot[dma] = self.dma_tot.get(dma, 0) + 16
            self.dma_serial[dma] = serial
            op.dval = self.dma_tot[dma]
        self.ops.append(op)
        return op

    def emit(self, sems, dsems):
        nc = self.nc
        for op in self.ops:
            for d, _ in op.deps:
                d.flag = True
        cnt = {}
        for op in self.ops:
            if op.dsem is None and op.flag:
                cnt[op.eng] = cnt.get(op.eng, 0) + 1
                op.val = cnt[op.eng]
        self.max_sem = dict(cnt)
        per = {}
        for op in self.ops:
            per.setdefault(op.eng, []).append(op)
        engmap = {"pe": "tensor", "act": "scalar", "dve": "vector", "pool": "gpsimd", "sp": "sync"}
        with nc.Block() as block:
            for en, lst in per.items():
                def body(e, lst=lst):
                    waited = {}
                    for op in lst:
                        need = {}
                        for d, ov in op.deps:
                            if d.dsem is not None:
                                s = dsems[d.dsem]
                                v = d.dval if ov is None else ov
                                key = ("d", d.dsem)
                            else:
                                s = sems[d.eng]
                                v = d.val
                                key = ("e", d.eng)
                            if waited.get(key, 0) >= v:
                                continue
                            if key not in need or need[key][1] < v:
                                need[key] = (s, v)
                        for key, (s, v) in need.items():
                            e.wait_ge(s, v)
                            waited[key] = v
                        ins = op.fn(e)
                        if op.dsem is not None:
                            ins.then_inc(dsems[op.dsem], 16)
                        elif op.flag:
                            ins.then_inc(sems[op.eng], 1)
                getattr(block, engmap[en])(body)


R_N1, R_BGLU, R_BGAT, R_DWB, R_LNW, R_LNB, R_BPW2, R_N2, R_DW, R_DNC = 0, 1, 3, 5, 6, 7, 8, 9, 10, 41
NROWS = 53


class _Stop(Exception):
    pass


def build(T, NSUB=4, dbg=False, limit=None):
    def stage(n):
        if limit is not None and n >= limit:
            raise _Stop()
    try:
        return _build(T, NSUB, stage)
    except _Stop:
        raise RuntimeError("unreachable")


def _build(T, NSUB, stage):
    import os
    SKIP = os.environ.get("SKIP", "")
    TT = 128 * NSUB
    NT = T // TT
    assert NT * TT == T
    nc = bass.Bass("TRN2", target_bir_lowering=False)
    P = Prog(nc)
    es = ExitStack()

    def din(name, shape):
        return nc.dram_tensor(name, list(shape), F32, kind="ExternalInput").ap()

    x = din("x", [T, D])
    norm1_w = din("norm1_w", [D])
    w_in = din("w_in", [D, INC])
    b_glu = din("b_glu", [2 * D])
    b_gates = din("b_gates", [2 * D])
    dn_conv_w = din("dn_conv_w", [4, 3 * D])
    dn_A_log = din("dn_A_log", [NH])
    dn_dt_bias = din("dn_dt_bias", [NH])
    dn_norm_w = din("dn_norm_w", [128])
    dn_w_o = din("dn_w_o", [D, D])
    cm_dw_w = din("cm_dw_w", [31, D])
    cm_dw_b = din("cm_dw_b", [D])
    cm_ln_w = din("cm_ln_w", [D])
    cm_ln_b = din("cm_ln_b", [D])
    cm_w_pw2 = din("cm_w_pw2", [D, D])
    cm_b_pw2 = din("cm_b_pw2", [D])
    w_out = din("w_out", [D, D])
    norm2_w = din("norm2_w", [D])
    ffn_gu = din("ffn_w_gate_up", [D, 2 * FH])
    ffn_dn = din("ffn_w_down", [FH, D])
    norm_f_w = din("norm_f_w", [D])
    out = nc.dram_tensor("out", [T, D], F32, kind="ExternalOutput").ap()

    def dscr(name, shape):
        return nc.dram_tensor(name, list(shape), BF16, kind="Internal").ap()

    wb_in = dscr("wb_in", [D, INC])
    wb_o = dscr("wb_o", [D, D])
    wb_pw2 = dscr("wb_pw2", [D, D])
    wb_out = dscr("wb_out", [D, D])
    wb_gu = dscr("wb_gu", [D, 2 * FH])
    wb_dn = dscr("wb_dn", [FH, D])

    def sb(name, shape, dt=F32):
        return es.enter_context(nc.sbuf_tensor(name, list(shape), dt))

    ident_b = sb("ident_b", [128, 128], BF16)
    ones_b = sb("ones_b", [128, 128], BF16)
    ones_f = sb("ones_f", [128, 128])
    idf = sb("idf", [128, 128])
    tri_f = sb("tri_f", [128, 128])
    chk_f = sb("chk_f", [128, 128])
    neg_t = sb("neg_t", [128, 128])
    pos_s = sb("pos_s", [128, 128])
    par = sb("par", [128, 8, NROWS])
    dnw = sb("dnw", [128, 1])
    dtb = sb("dtb", [128, NH])
    nega = sb("nega", [128, NH])
    nfw = sb("nfw", [128, D])
    wbd = sb("wbd", [128, 8, 16], BF16)

    XIN = [sb("xin%d" % s, [128, D]) for s in range(NSUB)]
    XS = [sb("xs%d" % s, [128, D]) for s in range(NSUB)]
    HB = [sb("hb%d" % i, [128, D], BF16) for i in range(2)]
    HT = sb("ht", [128, 8, TT], BF16)
    STT_ = sb("stt", [128, 8])
    PRE = [sb("pre%d" % i, [128, TT + 4], BF16) for i in range(2)]
    HALO = sb("halo", [128, 24, 4], BF16)
    NDG = 12
    DG = [sb("dg%d" % i, [128, 128], BF16) for i in range(NDG)]
    QKV = sb("qkv", [128, 24, TT], BF16)
    QN = QKV[:, 0:8, :]
    KN = QKV[:, 8:16, :]
    VS = QKV[:, 16:24, :]
    FF = QKV[:, 0:22, :]
    ZS = sb("zs", [128, 8, TT], BF16)
    CIN = sb("cin", [128, 8, 30 + TT], BF16)
    CN = CIN[:, :, 30:30 + TT]
    OG = sb("og", [128, 8, TT], BF16)
    MRG = ZS
    MA = [sb("ma%d" % i, [128, TT]) for i in range(2)]
    TMPW = [sb("tmpw%d" % i, [128, TT]) for i in range(2)]
    SQW = [sb("sqw%d" % i, [128, TT], BF16) for i in range(2)]
    SQL = [sb("sql%d" % i, [128, TT], BF16) for i in range(2)]
    SM = sb("sm", [128, NSUB, 12, NH])
    DSC = sb("dsc", [128, 4096])
    dsc3 = lambda i: DSC[:, i * 1024:(i + 1) * 1024].rearrange("p (h i) -> p h i", h=8)
    TRIG, EGB, XX, XA = dsc3(0), dsc3(1), dsc3(2), dsc3(3)
    CCV = DSC[:, 0:8 * TT].rearrange("p (c t) -> p c t", c=8)
    GATES = DSC[:, :].bitcast(BF16)[:, 0:16 * TT].rearrange("p (c t) -> p c t", c=16)
    RKN = ["trig", "egb", "xx", "xa"]

    def rk_ccv(c):
        names = sorted({RKN[(c * TT) // 1024], RKN[((c + 1) * TT - 1) // 1024]})
        return [(n, hs) for n in names for hs in (0, 1)]

    def rk_gates(c):
        return [(RKN[(c * TT // 2) // 1024], hs) for hs in (0, 1)]
    XB = sb("xb", [128, 8, 128])
    MMc = [sb("mm%d" % i, [128, 8, 128], BF16) for i in range(2)]
    NNc = [sb("nn%d" % i, [128, 8, 128], BF16) for i in range(2)]
    PF = sb("pf", [128, 8, 128])
    pfb = PF[:, :, :].rearrange("p h i -> p (h i)").bitcast(BF16)
    PBF = sb("pbf", [128, 8, 128], BF16)
    KBG = pfb[:, 0:1024].rearrange("p (h i) -> p h i", h=8)
    KD = pfb[:, 1024:2048].rearrange("p (h i) -> p h i", h=8)
    VB = sb("vb", [128, 8, 128], BF16)
    WT = sb("wt", [128, 8, 128], BF16)
    U = XB
    ATT = sb("att", [128, 8, 128], BF16)
    QD = sb("qd", [128, 8, 128], BF16)
    VN = sb("vn", [128, 8, 128], BF16)
    SQ = DSC[:, 0:512].bitcast(BF16).rearrange("p (h i) -> p h i", h=8)
    RR = TRIG
    T1 = WT
    S = sb("s", [128, 8, 128])
    SBF = sb("sbf", [128, 8, 128], BF16)
    NSLOT = 4
    WR = [sb("wr%d" % i, [128, 8, 512], BF16) for i in range(NSLOT)]
    rows = WR[3][:, :, :].rearrange("p k c -> p (k c)").bitcast(F32)[0:64, 0:D]

    PW = [es.enter_context(nc.psum_tensor("pw%d" % j, [128, 1024], F32)) for j in range(4)]

    def bank(b):
        return PW[b // 2][:, (b % 2) * 512:(b % 2) * 512 + 512]

    def bk(b):
        return bank(b)[:, 0:TT]

    def pw3(j):
        return PW[j][:, :].rearrange("p (h i) -> p h i", h=8)

    def pwb(j):
        return PW[j][:, 0:512].bitcast(BF16).rearrange("p (h i) -> p h i", h=8)

    sems = {k: es.enter_context(nc.semaphore("s_" + k)) for k in ("pe", "act", "dve", "pool")}
    dnames = ["setup", "c_in", "c_o", "c_pw2", "c_out", "c_gu", "c_dn"] + \
        ["w%d" % i for i in range(NSLOT)] + ["xl%d" % s for s in range(NSUB)] + ["st%d" % s for s in range(NSUB)]
    dsems = {k: es.enter_context(nc.semaphore("d_" + k)) for k in dnames}

    def MM(o, l, r, start, stop, rd, wr):
        P.add("pe", lambda e: e.matmul(out=o, lhsT=l, rhs=r, start=start, stop=stop), rd, wr)

    def TR(o, i, idn, rd, wr):
        P.add("pe", lambda e: e.transpose(out=o, in_=i, identity=idn), rd, wr)

    def ACT(o, i, func, rd, wr, bias=None, scale=None, accum=None):
        kw = {}
        if bias is not None:
            kw["bias"] = bias
        if scale is not None:
            kw["scale"] = scale
        if accum is not None:
            kw["accum_out"] = accum
        P.add("act", lambda e: e.activation(out=o, in_=i, func=func, **kw), rd, wr)

    def TS(eng, o, i, s1, s2, op0, op1, rd, wr):
        if op1 is None:
            P.add(eng, lambda e: e.tensor_scalar(out=o, in0=i, scalar1=s1, scalar2=None, op0=op0), rd, wr)
        else:
            P.add(eng, lambda e: e.tensor_scalar(out=o, in0=i, scalar1=s1, scalar2=s2, op0=op0, op1=op1), rd, wr)

    def TTo(eng, o, a, b, op, rd, wr):
        P.add(eng, lambda e: e.tensor_tensor(out=o, in0=a, in1=b, op=op), rd, wr)

    def STT(o, a, sc, b, op0, op1, rd, wr):
        P.add("dve", lambda e: e.scalar_tensor_tensor(out=o, in0=a, scalar=sc, in1=b, op0=op0, op1=op1), rd, wr)

    def CP(eng, o, i, rd, wr):
        if eng == "act":
            P.add("act", lambda e: e.activation(out=o, in_=i, func=AF.Copy), rd, wr)
        else:
            P.add(eng, lambda e: e.tensor_copy(out=o, in_=i), rd, wr)

    def MSET(eng, o, v, wr):
        P.add(eng, lambda e: e.memset(o, v), (), wr)

    def DMA(eng, o, i, rd, wr, sem, serial=True, slow=False):
        if slow:
            P.add(eng, lambda e: e.dma_start(out=o, in_=i, allow_slow_non_contiguous=True), rd, wr, dma=sem, serial=serial)
        else:
            P.add(eng, lambda e: e.dma_start(out=o, in_=i), rd, wr, dma=sem, serial=serial)

    def cast(src, dst, nrows, sem, key, rstep=128):
        if "cast" in SKIP:
            return []
        for r0 in range(0, nrows, rstep):
            r1 = min(nrows, r0 + rstep)
            DMA("pool", dst[r0:r1, :], src[r0:r1, :], (), [(key, r0)], sem, serial=False)
        return [(key, r0) for r0 in range(0, nrows, rstep)]

    K_in = cast(w_in, wb_in, D, "c_in", "wb_in")
    for s in range(NSUB):
        DMA("sp", XIN[s][:, :], x[s * 128:(s + 1) * 128, :], (), [("xin", s)], "xl%d" % s)
    MSET("dve", rows[:, :], 0.0, [("wr", 3)])
    prow = [(norm1_w, R_N1, 1), (b_glu, R_BGLU, 2), (b_gates, R_BGAT, 2), (cm_dw_b, R_DWB, 1), (cm_ln_w, R_LNW, 1),
            (cm_ln_b, R_LNB, 1), (cm_b_pw2, R_BPW2, 1), (norm2_w, R_N2, 1)]
    for ap_, r0, n in prow:
        DMA("sp", rows[r0:r0 + n, :], ap_.rearrange("(r c) -> r c", r=n), (), [("wr", 3)], "setup", serial=False)
    DMA("sp", rows[R_DW:R_DW + 31, :], cm_dw_w[:, :], (), [("wr", 3)], "setup", serial=False)
    for k in range(4):
        DMA("sp", rows[R_DNC + 3 * k:R_DNC + 3 * k + 3, :], dn_conv_w[k, :].rearrange("(r c) -> r c", r=3), (), [("wr", 3)],
            "setup", serial=False)
    DMA("sp", dnw[:, :], dn_norm_w.rearrange("(p o) -> p o", o=1), (), ["dnw"], "setup", serial=False)
    DMA("sp", dtb[:, :], dn_dt_bias.partition_broadcast(128), (), ["dtb"], "setup", serial=False)
    DMA("sp", nega[:, :], dn_A_log.partition_broadcast(128), (), ["nega"], "setup", serial=False)
    DMA("sp", nfw[:, :], norm_f_w.partition_broadcast(128), (), ["nfw"], "setup", serial=False)
    K_o = cast(dn_w_o, wb_o, D, "c_o", "wb_o", 256)
    K_pw2 = cast(cm_w_pw2, wb_pw2, D, "c_pw2", "wb_pw2", 256)
    K_out = cast(w_out, wb_out, D, "c_out", "wb_out", 256)
    K_gu = cast(ffn_gu, wb_gu, D, "c_gu", "wb_gu")
    K_dn = cast(ffn_dn, wb_dn, FH, "c_dn", "wb_dn", 256)
    DMA("sp", wbd[:, :, :], wb_in[:, 4096:4112].rearrange("(k p) c -> p k c", p=128), K_in, ["wbd"], "setup",
        serial=False, slow=True)

    MSET("pool", ones_f[:, :], 1.0, ["ones_f"])
    MSET("pool", ones_b[:, :], 1.0, ["ones_b"])
    MSET("pool", idf[:, :], 0.0, ["idf"])
    P.add("pool", lambda e: e.affine_select(out=idf[:, :], in_=idf[:, :], pattern=[[-1, 128]], compare_op=ALU.not_equal,
                                            fill=1.0, base=0, channel_multiplier=1), ["idf"], ["idf"])
    CP("pool", ident_b[:, :], idf[:, :], ["idf"], ["ident_b"])
    P.add("pool", lambda e: e.affine_select(out=tri_f[:, :], in_=ones_f[:, :], pattern=[[1, 128]], compare_op=ALU.is_ge,
                                            fill=0.0, base=0, channel_multiplier=-1), ["ones_f"], ["tri_f"])
    MSET("pool", tri_f[0:64, 64:128], 0.0, ["tri_f"])
    CP("pool", chk_f[:, :], ones_f[:, :], ["ones_f"], ["chk_f"])
    MSET("pool", chk_f[0:64, 64:128], 0.0, ["chk_f"])
    MSET("pool", chk_f[64:128, 0:64], 0.0, ["chk_f"])
    TS("pool", neg_t[:, :], tri_f[:, :], -1.0, BIG, ALU.add, ALU.mult, ["tri_f"], ["neg_t"])
    P.add("pool", lambda e: e.affine_select(out=pos_s[:, :], in_=ones_f[:, :], pattern=[[-1, 128]], compare_op=ALU.is_gt,
                                            fill=0.0, base=0, channel_multiplier=1), ["ones_f"], ["pos_s"])
    MSET("pool", pos_s[64:128, 0:64], 0.0, ["pos_s"])
    TS("pool", pos_s[:, :], pos_s[:, :], -BIG, BIG, ALU.mult, ALU.add, ["pos_s"], ["pos_s"])
    MSET("pool", HALO[:, :, :], 0.0, ["halo"])
    MSET("pool", CIN[:, :, 0:30], 0.0, [("cin", c) for c in range(8)])
    MSET("pool", S[:, :, :], 0.0, [("S", h) for h in range(NH)])
    MSET("pool", SBF[:, :, :], 0.0, [("SBF", h) for h in range(NH)])
    for c in range(8):
        TR(bank(0)[:, c * 64:c * 64 + NROWS], rows[0:NROWS, c * 128:(c + 1) * 128], idf[0:NROWS, 0:NROWS],
           [("wr", 3), "idf"], [("ps", 0)])
    CP("dve", par[:, :, :], bank(0).rearrange("p (c r) -> p c r", c=8)[:, :, 0:NROWS], [("ps", 0)], ["par"])
    ACT(nega[:, :], nega[:, :], AF.Exp, ["nega"], ["nega"])
    TS("dve", nega[:, :], nega[:, :], -1.0, None, ALU.mult, None, ["nega"], ["nega"])

    loads = []

    def plan_tile():
        L = []
        for g in range(6):
            L.append((wb_in, 0, 8, g * 512, 512, K_in))
        for g in range(2):
            L.append((wb_in, 0, 8, 4112 + g * 512, 512, K_in))
            L.append((wb_in, 0, 8, 4112 + 1024 + g * 512, 512, K_in))
        for g in range(2):
            L.append((wb_in, 0, 8, 3072 + g * 512, 512, K_in))
        for g in range(4):
            L.append((wb_in, 0, 8, 6160 + g * 512, 512, K_in))
        for g in range(2):
            L.append((wb_o, 0, 8, g * 512, 512, K_o))
        for g in range(2):
            L.append((wb_pw2, 0, 8, g * 512, 512, K_pw2))
        for g in range(2):
            L.append((wb_out, 0, 8, g * 512, 512, K_out))
        for g in range(6):
            nc_ = 512 if g < 5 else 256
            L.append((wb_gu, 0, 8, g * 512, nc_, K_gu))
            L.append((wb_gu, 0, 8, FH + g * 512, nc_, K_gu))
        for hf in range(2):
            for kg in range(3):
                nk = 8 if kg < 2 else 6
                L.append((wb_dn, kg * 1024, nk, hf * 512, 512, K_dn))
        return L

    for t in range(NT):
        loads.extend(plan_tile())
    issued = [0]
    consumed = [0]

    def issue_to(n):
        while issued[0] < min(n, len(loads)):
            m = issued[0]
            dr, r0, nk, c0, ncol, ck = loads[m]
            slot = m % NSLOT
            src = dr[r0:r0 + 128 * nk, c0:c0 + ncol].rearrange("(k p) c -> p k c", p=128)
            DMA("sp", WR[slot][:, 0:nk, 0:ncol], src, ck, [("wr", slot)], "w%d" % slot)
            issued[0] += 1

    def next_w():
        n = consumed[0]
        consumed[0] += 1
        assert issued[0] >= n + 1 or n < NSLOT + 100000
        issue_to(n + 1)
        return n % NSLOT

    def fin():
        issue_to(consumed[0] + NSLOT)

    dgc = [0]
    DG_ENG = ("dve", "pool")

    def build_diag(sc):
        i = dgc[0] % NDG
        eng = DG_ENG[dgc[0] % len(DG_ENG)]
        dgc[0] += 1
        if eng == "act":
            ACT(DG[i][:, :], ident_b[:, :], AF.Identity, ["ident_b", "par"], [("dg", i)], scale=sc)
        else:
            TS(eng, DG[i][:, :], ident_b[:, :], sc, 0.0, ALU.mult, ALU.add, ["ident_b", "par"], [("dg", i)])
        return DG[i][:, :], ("dg", i)

    def tok_rstd(src, junk, jkey, col, rd, scale_n):
        ACT(junk, src, AF.Square, rd, [jkey, ("st", col)], accum=STT_[:, col:col + 1])
        TS("dve", STT_[:, col:col + 1], STT_[:, col:col + 1], scale_n, EPS, ALU.mult, ALU.add, [("st", col)], [("st", col)])
        ACT(STT_[:, col:col + 1], STT_[:, col:col + 1], AF.Ln, [("st", col)], [("st", col)])
        ACT(STT_[:, col:col + 1], STT_[:, col:col + 1], AF.Exp, [("st", col)], [("st", col)], scale=-0.5)

    def norm_stats(s, src, skey):
        tok_rstd(src[:, :], HB[s % 2][:, :], ("hb", s % 2), s, [skey], 1.0 / D)

    def norm_apply(s, src, skey, nrow, pb0):
        hb = HB[s % 2]
        TS("dve", hb[:, :], src[:, :], STT_[:, s:s + 1], None, ALU.mult, None, [skey, ("st", s)], [("hb", s % 2)])
        for c in range(8):
            TR(bank(pb0 + c // 4).bitcast(BF16)[:, (c % 4) * 128:(c % 4) * 128 + 128],
               hb[:, c * 128:(c + 1) * 128], ident_b[:, :], [("hb", s % 2), "ident_b"], [("ps", pb0 + c // 4)])
        for hf in range(2):
            srcp = bank(pb0 + hf).bitcast(BF16)[:, 0:512].rearrange("p (c i) -> p c i", c=4)
            TTo("dve", HT[:, hf * 4:hf * 4 + 4, s * 128:(s + 1) * 128], srcp,
                par[:, hf * 4:hf * 4 + 4, nrow:nrow + 1].to_broadcast([128, 4, 128]), ALU.mult,
                [("ps", pb0 + hf), "par"], [("ht", s)])

    HTK = [("ht", s) for s in range(NSUB)]
    subs = lambda s: slice(s * 128, (s + 1) * 128)

    try:
      stage(0)
      for t in range(NT):
        tok0 = t * TT
        if t == 0:
            for s in range(NSUB):
                norm_stats(s, XIN[s], ("xin", s))
            for s in range(NSUB):
                norm_apply(s, XIN[s], ("xin", s), R_N1, 0)

        def tile_smalls():
            for s in range(NSUB):
                for kc in range(8):
                    MM(bank(2)[:, s * 16:s * 16 + 16], HT[:, kc, subs(s)], wbd[:, kc, :], kc == 0, kc == 7,
                       [("ht", s), "wbd"], [("ps", 2)])
            bdp = bank(2)[:, 0:NSUB * 16].rearrange("p (s c) -> p s c", s=NSUB)
            ACT(SM[:, :, 0, :], bdp[:, :, 0:8], AF.Sigmoid, [("ps", 2)], ["sm"])
            TTo("dve", SM[:, :, 1, :], bdp[:, :, 8:16], dtb[:, :].unsqueeze(1).to_broadcast([128, NSUB, NH]), ALU.add,
                [("ps", 2), "dtb"], ["sm"])


            ACT(SM[:, :, 2, :], SM[:, :, 1, :], AF.Exp, ["sm"], ["sm"])
            ACT(SM[:, :, 3, :], SM[:, :, 2, :], AF.Ln, ["sm"], ["sm"], bias=1.0)
            TTo("dve", SM[:, :, 4, :], SM[:, :, 3, :], nega[:, :].unsqueeze(1).to_broadcast([128, NSUB, NH]), ALU.mult,
                ["sm", "nega"], ["sm"])
            for s in range(NSUB):
                MM(bank(2)[:, 64 + s * 16:64 + s * 16 + 8], tri_f[:, :], SM[:, s, 4, :], True, True, ["tri_f", "sm"], [("ps", 2)])
                MM(bank(2)[:, 64 + s * 16 + 8:64 + s * 16 + 16], chk_f[:, :], SM[:, s, 4, :], True, True, ["chk_f", "sm"], [("ps", 2)])
            gcp = bank(2)[:, 64:64 + NSUB * 16].rearrange("p (s c) -> p s c", s=NSUB)
            CP("dve", SM[:, :, 5, :], gcp[:, :, 0:8], [("ps", 2)], ["sm"])
            CP("dve", SM[:, :, 6, :], gcp[:, :, 8:16], [("ps", 2)], ["sm"])
            ACT(SM[:, :, 7, :], SM[:, :, 5, :], AF.Exp, ["sm"], ["sm"])
            TTo("dve", SM[:, :, 8, :], SM[:, :, 0, :], SM[:, :, 7, :], ALU.mult, ["sm"], ["sm"])
            TTo("dve", SM[:, :, 9, :], SM[:, :, 6, :], SM[:, :, 5, :], ALU.subtract, ["sm"], ["sm"])
            ACT(SM[:, :, 10, :], SM[:, :, 9, :], AF.Exp, ["sm"], ["sm"])


        stage(1)
        if t == 0:
            tile_smalls()
        stage(2)
        def qkv_post(ch):
            seg, hh = ch // 8, ch % 8
            pbk = ch % 2
            pre = PRE[ch % 2]
            CP("pool", pre[:, 0:4], HALO[:, ch, :], ["halo"], [("pre", ch % 2)])
            ACT(pre[:, 4:4 + TT], bk(pbk), AF.Identity, [("ps", pbk)], [("pre", ch % 2)])
            CP("pool", HALO[:, ch, :], pre[:, TT:TT + 4], [("pre", ch % 2)], ["halo"])
            for k in range(4):
                dg, dk_ = build_diag(par[:, ch % 8, R_DNC + 3 * k + seg:R_DNC + 3 * k + seg + 1])
                MM(bk(2 + pbk), dg, pre[:, 1 + k:1 + k + TT], k == 0, k == 3, [dk_, ("pre", ch % 2)], [("ps", 2 + pbk)])
            dst = (QN, KN, VS)[seg]
            ACT(dst[:, hh, :], bk(2 + pbk), AF.Silu, [("ps", 2 + pbk)], [("qkv", seg, hh)])

        for g in range(6):
            slot = next_w()
            for j in range(4):
                ch = g * 4 + j
                pbk = ch % 2
                for kc in range(8):
                    MM(bk(pbk), WR[slot][:, kc, j * 128:(j + 1) * 128], HT[:, kc, :], kc == 0, kc == 7,
                       HTK + [("wr", slot)], [("ps", pbk)])
                if ch >= 1:
                    qkv_post(ch - 1)
            fin()
        qkv_post(23)
        stage(3)
        stage(4)
        for g in range(2):
            sa = next_w()
            sbk = next_w()
            for j in range(4):
                c = g * 4 + j
                ba, bb = 2 * (c % 2), 2 * (c % 2) + 1
                for kc in range(8):
                    MM(bk(ba), WR[sa][:, kc, j * 128:(j + 1) * 128], HT[:, kc, :], kc == 0, kc == 7,
                       HTK + [("wr", sa)], [("ps", ba)])
                for kc in range(8):
                    MM(bk(bb), WR[sbk][:, kc, j * 128:(j + 1) * 128], HT[:, kc, :], kc == 0, kc == 7,
                       HTK + [("wr", sbk)], [("ps", bb)])
                tw = TMPW[c % 2]
                ACT(tw[:, :], bk(bb), AF.Sigmoid, [("ps", bb), "par"], [("tmpw", c % 2)], bias=par[:, c, R_BGLU + 1:R_BGLU + 2])
                STT(CIN[:, c, 30:30 + TT], bk(ba), par[:, c, R_BGLU:R_BGLU + 1], tw[:, :], ALU.add, ALU.mult,
                    [("ps", ba), "par", ("tmpw", c % 2)], [("cin", c)])
            fin()
        stage(7)
        stage(8)
        def l2_sq(seg, buf, hh, i2):
            TTo("dve", SQW[i2][:, :], buf[:, hh, :], buf[:, hh, :], ALU.mult, [("qkv", seg, hh)], [("sqw", i2)])

        def l2_mm(i2):
            MM(bk(4 + i2), ones_b[:, :], SQW[i2][:, :], True, True, [("sqw", i2), "ones_b"], [("ps", 4 + i2)])
            ACT(TMPW[i2][:, :], bk(4 + i2), AF.Ln, [("ps", 4 + i2)], [("tmpw", i2)], bias=EPS)
            ACT(TMPW[i2][:, :], TMPW[i2][:, :], AF.Exp, [("tmpw", i2)], [("tmpw", i2)], scale=-0.5)

        def l2_mul(seg, buf, hh, i2):
            TTo("dve", buf[:, hh, :], buf[:, hh, :], TMPW[i2][:, :], ALU.mult, [("qkv", seg, hh), ("tmpw", i2)],
                [("qkv", seg, hh)])

        def ln_cast(c):
            ACT(SQL[0][:, :], CCV[:, c, :], AF.Copy, [*rk_ccv(c)], [("sql", 0)])
            ACT(SQL[1][:, :], CCV[:, c, :], AF.Square, [*rk_ccv(c)], [("sql", 1)])

        def ln_mm(c, which):
            MM(bk(6 + which), ones_b[:, :], SQL[which][:, :], c == 0, c == 7, [("sql", which), "ones_b"], [("ps", 6 + which)])

        for c in range(8):
            pbk = c % 2
            for k in range(31):
                dg, dk_ = build_diag(par[:, c, R_DW + k:R_DW + k + 1])
                MM(bk(pbk), dg, CIN[:, c, k:k + TT], k == 0, k == 30, [dk_, ("cin", c)], [("ps", pbk)])
                if k == 2:
                    l2_sq(0, QN, c, 0)
                elif k == 4:
                    l2_sq(1, KN, c, 1)
                elif k == 6 and c >= 1:
                    ln_cast(c - 1)
                elif k == 10:
                    l2_mm(0)
                elif k == 14:
                    l2_mm(1)
                elif k == 17 and c >= 1:
                    ln_mm(c - 1, 0)
                elif k == 20 and c >= 1:
                    ln_mm(c - 1, 1)
                elif k == 24:
                    l2_mul(0, QN, c, 0)
                elif k == 28:
                    l2_mul(1, KN, c, 1)
            ACT(CCV[:, c, :], bk(pbk), AF.Identity, [("ps", pbk), "par"], [*rk_ccv(c)], bias=par[:, c, R_DWB:R_DWB + 1])
            CP("pool", CIN[:, c, 0:30], CIN[:, c, TT:TT + 30], [("cin", c)], [("cin", c)])
        ln_cast(7)
        ln_mm(7, 0)
        ln_mm(7, 1)
        stage(9)
        TS("dve", MA[0][:, :], bk(6), 1.0 / D, None, ALU.mult, None, [("ps", 6)], [("ma", 0)])
        TTo("dve", TMPW[0][:, :], MA[0][:, :], MA[0][:, :], ALU.mult, [("ma", 0)], [("tmpw", 0)])
        STT(MA[1][:, :], bk(7), 1.0 / D, TMPW[0][:, :], ALU.mult, ALU.subtract, [("ps", 7), ("tmpw", 0)], [("ma", 1)])
        ACT(MA[1][:, :], MA[1][:, :], AF.Ln, [("ma", 1)], [("ma", 1)], bias=EPS)
        ACT(MA[1][:, :], MA[1][:, :], AF.Exp, [("ma", 1)], [("ma", 1)], scale=-0.5)
        for g in range(2):
            slot = next_w()
            for j in range(4):
                c = g * 4 + j
                pbk = j % 2
                for kc in range(8):
                    MM(bk(pbk), WR[slot][:, kc, j * 128:(j + 1) * 128], HT[:, kc, :], kc == 0, kc == 7,
                       HTK + [("wr", slot)], [("ps", pbk)])
                ACT(ZS[:, c, :], bk(pbk), AF.Silu, [("ps", pbk)], [("zs", c)])
                TTo("dve", CCV[:, c, :], CCV[:, c, :], MA[0][:, :], ALU.subtract, [*rk_ccv(c), ("ma", 0)], [*rk_ccv(c)])
                TTo("pool" if c % 3 == 2 else "dve", CCV[:, c, :], CCV[:, c, :], MA[1][:, :], ALU.mult, [*rk_ccv(c), ("ma", 1)],
                    [*rk_ccv(c)])
                ACT(CN[:, c, :], CCV[:, c, :], AF.Silu, [*rk_ccv(c), "par"], [("cin", c)],
                    bias=par[:, c, R_LNB:R_LNB + 1], scale=par[:, c, R_LNW:R_LNW + 1])
            fin()

        stage(10)
        HS = (0, 1)

        def hsl(hs):
            return slice(4 * hs, 4 * hs + 4)

        def hrange(hs):
            return range(4 * hs, 4 * hs + 4)

        def pwbh(j, hs):
            return bank(2 * j + hs).bitcast(BF16)[:, 0:512].rearrange("p (h i) -> p h i", h=4)

        def bc4(v, hs):
            return v[:, hsl(hs)].unsqueeze(2).to_broadcast([128, 4, 128])

        def mb4(m):
            return m[:, :].unsqueeze(1).to_broadcast([128, 4, 128])

        def d_prep(s):
            g_, beta_, gc_ = SM[:, s, 4, :], SM[:, s, 0, :], SM[:, s, 5, :]
            for hs in HS:
                TTo("dve", TRIG[:, hsl(hs), :], mb4(tri_f), bc4(g_, hs), ALU.mult, ["tri_f", "sm"], [("trig", hs)])
            for hs in HS:
                for h in hrange(hs):
                    MM(pw3(1)[:, h, :], ones_f[:, :], TRIG[:, h, :], True, True, [("trig", hs), "ones_f"], [("ps", 2 + hs)])
            for hs in HS:
                ACT(EGB[:, hsl(hs), :], pw3(1)[:, hsl(hs), :], AF.Exp, [("ps", 2 + hs)], [("egb", hs)])
                TTo("dve", XX[:, hsl(hs), :], pw3(1)[:, hsl(hs), :], bc4(gc_, hs), ALU.subtract, [("ps", 2 + hs), "sm"], [("xx", hs)])
            for hs in HS:
                TTo("dve", XB[:, hsl(hs), :], XX[:, hsl(hs), :], mb4(pos_s), ALU.add, [("xx", hs), "pos_s"], [("xb", hs)])
            for hs in HS:
                ACT(XB[:, hsl(hs), :], XB[:, hsl(hs), :], AF.Exp, [("xb", hs)], [("xb", hs)], scale=-1.0)
            for hs in HS:
                TTo("dve", XB[:, hsl(hs), :], XB[:, hsl(hs), :], bc4(beta_, hs), ALU.mult, [("xb", hs), "sm"], [("xb", hs)])

        def d_head(s):
            sl = subs(s)
            stage(11)
            for hs in HS:
                for h in hrange(hs):
                    MM(pw3(2)[:, h, :], KN[:, h, sl], KN[:, h, sl], True, True, [("qkv", 1, h)], [("ps", 4 + hs)])
            for hs in HS:
                TTo("dve", MMc[0][:, hsl(hs), :], pw3(2)[:, hsl(hs), :], XB[:, hsl(hs), :], ALU.mult, [("ps", 4 + hs), ("xb", hs)],
                    [("mm", 0, hs)])
            for hs in HS:
                for h in hrange(hs):
                    TR(pwbh(3, hs)[:, h - 4 * hs, :], MMc[0][:, h, :], ident_b[:, :], [("mm", 0, hs), "ident_b"], [("ps", 6 + hs)])
            for hs in HS:
                CP("act", NNc[0][:, hsl(hs), :], pwbh(3, hs), [("ps", 6 + hs)], [("nn", 0, hs)])
                STT(PBF[:, hsl(hs), :], pwbh(3, hs), -1.0, mb4(idf), ALU.mult, ALU.add, [("ps", 6 + hs), "idf"], [("pbf", hs)])

        def d_tail(s):
            sl = subs(s)
            beta_ = SM[:, s, 0, :]
            stage(12)
            for hs in HS:
                TTo("pool", XA[:, hsl(hs), :], XX[:, hsl(hs), :], mb4(neg_t), ALU.add, [("xx", hs), "neg_t"], [("xa", hs)])
                ACT(XA[:, hsl(hs), :], XA[:, hsl(hs), :], AF.Exp, [("xa", hs)], [("xa", hs)])
            for l in range(1, 6):
                cur, nxt = (l - 1) % 2, l % 2
                if l == 2:
                    for hs in HS:
                        STT(QD[:, hsl(hs), :], QN[:, hsl(hs), sl], 128.0 ** -0.5, EGB[:, hsl(hs), :], ALU.mult, ALU.mult,
                            [("qkv", 0, h) for h in hrange(hs)] + [("egb", hs)], [("qd", hs)])
                if l == 5:
                    for hs in HS:
                        for h in hrange(hs):
                            TR(pwbh(2, hs)[:, h - 4 * hs, :], VS[:, h, sl], ident_b[:, :], [("qkv", 2, h), "ident_b"], [("ps", 4 + hs)])
                    for hs in HS:
                        TTo("dve", VB[:, hsl(hs), :], pwbh(2, hs), bc4(beta_, hs), ALU.mult, [("ps", 4 + hs), "sm"], [("vb", hs)])
                for hs in HS:
                    for h in hrange(hs):
                        MM(pw3(1)[:, h, :], NNc[cur][:, h, :], MMc[cur][:, h, :], True, True, [("nn", cur, hs), ("mm", cur, hs)],
                           [("ps", 2 + hs)])
                    if l < 5:
                        for h in hrange(hs):
                            MM(pw3(2)[:, h, :], MMc[cur][:, h, :], NNc[cur][:, h, :], True, True, [("nn", cur, hs), ("mm", cur, hs)],
                               [("ps", 4 + hs)])
                for hs in HS:
                    CP("act", MMc[nxt][:, hsl(hs), :], pw3(1)[:, hsl(hs), :], [("ps", 2 + hs)], [("mm", nxt, hs)])
                    if l < 5:
                        CP("act" if hs == 0 else "dve", NNc[nxt][:, hsl(hs), :], pw3(2)[:, hsl(hs), :], [("ps", 4 + hs)], [("nn", nxt, hs)])
                for hs in HS:
                    for h in hrange(hs):
                        MM(pw3(3)[:, h, :], MMc[nxt][:, h, :], PBF[:, h, :], True, False, [("mm", nxt, hs), ("pbf", hs)], [("ps", 6 + hs)])
                        MM(pw3(3)[:, h, :], ident_b[:, :], PBF[:, h, :], False, True, ["ident_b", ("pbf", hs)], [("ps", 6 + hs)])
                for hs in HS:
                    CP("dve", PBF[:, hsl(hs), :], pw3(3)[:, hsl(hs), :], [("ps", 6 + hs)], [("pbf", hs)])
                if l == 4 and pending_onorm[0] is not None:
                    d_onorm(pending_onorm[0])
                    pending_onorm[0] = None
            stage(13)
            for hs in HS:
                for h in hrange(hs):
                    TR(pwbh(1, hs)[:, h - 4 * hs, :], KN[:, h, sl], ident_b[:, :], [("qkv", 1, h), "ident_b"], [("ps", 2 + hs)])
            for hs in HS:
                TTo("dve", KBG[:, hsl(hs), :], pwbh(1, hs), bc4(SM[:, s, 8, :], hs), ALU.mult, [("ps", 2 + hs), "sm"], [("pf", hs)])
                TTo("dve", KD[:, hsl(hs), :], pwbh(1, hs), bc4(SM[:, s, 10, :], hs), ALU.mult, [("ps", 2 + hs), "sm"], [("pf", hs)])
            stage(14)
            for hs in HS:
                for h in hrange(hs):
                    MM(pw3(3)[:, h, :], KBG[:, h, :], PBF[:, h, :], True, True, [("pf", hs), ("pbf", hs)], [("ps", 6 + hs)])
            for hs in HS:
                CP("act", WT[:, hsl(hs), :], pw3(3)[:, hsl(hs), :], [("ps", 6 + hs)], [("wt", hs)])
            for hs in HS:
                for h in hrange(hs):
                    MM(pw3(1)[:, h, :], PBF[:, h, :], VB[:, h, :], True, True, [("vb", hs), ("pbf", hs)], [("ps", 2 + hs)])
            for hs in HS:
                CP("act", U[:, hsl(hs), :], pw3(1)[:, hsl(hs), :], [("ps", 2 + hs)], [("xb", hs)])
            stage(15)
            for hs in HS:
                for h in hrange(hs):
                    MM(pw3(2)[:, h, :], KN[:, h, sl], QN[:, h, sl], True, True, [("qkv", 1, h), ("qkv", 0, h)], [("ps", 4 + hs)])
            for hs in HS:
                STT(ATT[:, hsl(hs), :], pw3(2)[:, hsl(hs), :], 128.0 ** -0.5, XA[:, hsl(hs), :], ALU.mult, ALU.mult,
                    [("ps", 4 + hs), ("xa", hs)], [("att", hs)])
            stage(16)
            for chn in range(2):
                r0 = 64 * chn
                rs = slice(r0, r0 + 64)
                for h in range(NH):
                    MM(pw3(3)[:, h, :], WT[:, h, :], SBF[:, h, :], True, True, [("wt", h // 4), ("SBF", h)], [("ps", 6 + h // 4)])
                for hs in HS:
                    TTo("dve", VN[rs, hsl(hs), :], U[rs, hsl(hs), :], pw3(3)[rs, hsl(hs), :], ALU.subtract, [("xb", hs), ("ps", 6 + hs)],
                        [("vn", chn, hs)])
                for h in range(NH):
                    MM(pw3(1)[:, h, :], KD[rs, h, :], VN[rs, h, :], True, True, [("pf", h // 4), ("vn", chn, h // 4)], [("ps", 2 + h // 4)])
                for h in range(NH):
                    MM(pw3(0)[:, h, rs], SBF[:, h, :], QD[:, h, rs], True, False, [("qd", h // 4), ("SBF", h)], [("ps", h // 4)])
                    MM(pw3(0)[:, h, rs], VN[rs, h, :], ATT[rs, h, rs], False, True, [("vn", chn, h // 4), ("att", h // 4)], [("ps", h // 4)])
                for h in range(NH):
                    STT(S[:, h, :], S[:, h, :], EGB[:, h, r0 + 63:r0 + 64], pw3(1)[:, h, :], ALU.mult, ALU.add,
                        [("S", h), ("egb", h // 4), ("ps", 2 + h // 4)], [("S", h)])
                    CP("act" if h % 2 == 0 else "pool", SBF[:, h, :], S[:, h, :], [("S", h)], [("SBF", h)])

        def d_onorm(s):
            sl = subs(s)
            stage(17)
            TK = [("trig", 0), ("trig", 1)]
            WK = [("wt", 0), ("wt", 1)]
            ACT(SQ[:, :, :], pw3(0), AF.Square, [("ps", 0), ("ps", 1)], TK)
            for h in range(NH):
                MM(pw3(2)[:, h, :], ones_b[:, :], SQ[:, h, :], True, True, TK + ["ones_b"], [("ps", 4 + h // 4)])
            ACT(RR[:, :, :], pw3(2), AF.Ln, [("ps", 4), ("ps", 5)], TK, bias=EPS, scale=1.0 / 128)
            ACT(RR[:, :, :], RR[:, :, :], AF.Exp, TK, TK, scale=-0.5)
            STT(T1[:, :, :], pw3(0), dnw[:, 0:1], RR[:, :, :], ALU.mult, ALU.mult, [("ps", 0), ("ps", 1), "dnw"] + TK, WK)
            TTo("dve", OG[:, :, sl], T1[:, :, :], ZS[:, :, sl], ALU.mult, WK + [("zs", c) for c in range(8)], [("og", s)])

        pending_onorm = [None]
        d_prep(0)
        d_head(0)
        for s in range(NSUB):
            d_tail(s)
            if s + 1 < NSUB:
                d_prep(s + 1)
                d_head(s + 1)
                pending_onorm[0] = s
            else:
                d_onorm(s)

        stage(5)
        for g in range(4):
            slot = next_w()
            for j in range(4):
                c = g * 4 + j
                pbk = j % 2
                for kc in range(8):
                    MM(bk(pbk), WR[slot][:, kc, j * 128:(j + 1) * 128], HT[:, kc, :], kc == 0, kc == 7,
                       HTK + [("wr", slot)], [("ps", pbk)])
                ACT(GATES[:, c, :], bk(pbk), AF.Sigmoid, [("ps", pbk), "par"], [*rk_gates(c)],
                    bias=par[:, c % 8, R_BGAT + c // 8:R_BGAT + c // 8 + 1])
            fin()

        stage(21)
        OGK = [("og", s) for s in range(NSUB)]
        for g in range(2):
            slot = next_w()
            for j in range(4):
                c = g * 4 + j
                pbk = j % 2
                for kc in range(8):
                    MM(bk(pbk), WR[slot][:, kc, j * 128:(j + 1) * 128], OG[:, kc, :], kc == 0, kc == 7,
                       OGK + [("wr", slot)], [("ps", pbk)])
                TTo("dve", MRG[:, c, :], bk(pbk), GATES[:, c, :], ALU.mult, [("ps", pbk), *rk_gates(c)], [("zs", c)])
            fin()
        for g in range(2):
            slot = next_w()
            for j in range(4):
                c = g * 4 + j
                pbk = j % 2
                for kc in range(8):
                    MM(bk(pbk), WR[slot][:, kc, j * 128:(j + 1) * 128], CN[:, kc, :], kc == 0, kc == 7,
                       [("cin", k) for k in range(8)] + [("wr", slot)], [("ps", pbk)])
                STT(MA[pbk][:, :], bk(pbk), par[:, c, R_BPW2:R_BPW2 + 1], GATES[:, 8 + c, :], ALU.add, ALU.mult,
                    [("ps", pbk), "par", *rk_gates(8 + c)], [("ma", pbk)])
                TTo("pool", MRG[:, c, :], MRG[:, c, :], MA[pbk][:, :], ALU.add, [("zs", c), ("ma", pbk)], [("zs", c)])
            fin()
        stage(22)
        MK = [("zs", c) for c in range(8)]
        wslots = [next_w(), next_w()]
        for s in range(NSUB):
            for hf in range(2):
                pbk = (2 * s + hf) % 4
                for kc in range(8):
                    MM(bank(pbk), MRG[:, kc, subs(s)], WR[wslots[hf]][:, kc, :], kc == 0, kc == 7,
                       MK + [("wr", wslots[hf])], [("ps", pbk)])
                TTo("dve", XS[s][:, hf * 512:(hf + 1) * 512], XIN[s][:, hf * 512:(hf + 1) * 512], bank(pbk), ALU.add,
                    [("xin", s), ("ps", pbk)], [("xs", s)])
            norm_stats(s, XS[s], ("xs", s))
            if s >= 1:
                norm_apply(s - 1, XS[s - 1], ("xs", s - 1), R_N2, 4)
        fin()
        if t + 1 < NT:
            for s in range(NSUB):
                DMA("sp", XIN[s][:, :], x[tok0 + TT + s * 128:tok0 + TT + (s + 1) * 128, :], (), [("xin", s)], "xl%d" % s)
        stage(23)
        norm_apply(NSUB - 1, XS[NSUB - 1], ("xs", NSUB - 1), R_N2, 4)
        stage(24)
        for g in range(6):
            sg = next_w()
            su = next_w()
            nj = 4 if g < 5 else 2
            for j in range(nj):
                fc = g * 4 + j
                bg_, bu_ = 2 * (j % 2), 2 * (j % 2) + 1
                for kc in range(8):
                    MM(bk(bg_), WR[sg][:, kc, j * 128:(j + 1) * 128], HT[:, kc, :], kc == 0, kc == 7, HTK + [("wr", sg)], [("ps", bg_)])
                for kc in range(8):
                    MM(bk(bu_), WR[su][:, kc, j * 128:(j + 1) * 128], HT[:, kc, :], kc == 0, kc == 7, HTK + [("wr", su)], [("ps", bu_)])
                tw = TMPW[fc % 2]
                ACT(tw[:, :], bk(bg_), AF.Silu, [("ps", bg_)], [("tmpw", fc % 2)])
                TTo("dve", FF[:, fc, :], tw[:, :], bk(bu_), ALU.mult, [("tmpw", fc % 2), ("ps", bu_)], [("qkv", fc // 8, fc % 8)])
            fin()
        stage(25)
        if t + 1 < NT:
            for s in range(NSUB):
                norm_stats(s, XIN[s], ("xin", s))
        FK = [("qkv", fc // 8, fc % 8) for fc in range(22)]
        for hf in range(2):
            for kg in range(3):
                slot = next_w()
                nk = 8 if kg < 2 else 6
                for s in range(NSUB):
                    pbk = 4 + s
                    for kk in range(nk):
                        fc = kg * 8 + kk
                        MM(bank(pbk), FF[:, fc, subs(s)], WR[slot][:, kk, :], fc == 0, fc == 21,
                           [FK[fc], ("wr", slot)], [("ps", pbk)])
                    if kg == 0 and hf == 0 and t + 1 < NT:
                        norm_apply(s, XIN[s], ("xin", s), R_N1, 0)
                    if kg == 1 and hf == 0 and s == NSUB - 1 and t + 1 < NT:
                        tile_smalls()
                    if kg == 2:
                        TTo("dve", XS[s][:, hf * 512:(hf + 1) * 512], XS[s][:, hf * 512:(hf + 1) * 512], bank(pbk), ALU.add,
                            [("xs", s), ("ps", pbk)], [("xs", s)])
                        if hf == 1:
                            hb = HB[s % 2]
                            tok_rstd(XS[s][:, :], hb[:, :], ("hb", s % 2), 4 + s, [("xs", s)], 1.0 / D)
                            STT(XS[s][:, :], XS[s][:, :], STT_[:, 4 + s:5 + s], nfw[:, :], ALU.mult, ALU.mult,
                                [("xs", s), ("st", 4 + s), "nfw"], [("xs", s)])
                            DMA("pool", out[tok0 + s * 128:tok0 + (s + 1) * 128, :], XS[s][:, :], [("xs", s)], [("out", s)],
                                "st%d" % s)
                fin()
        stage(26)
    except _Stop:
        pass
    P.add("pool", lambda e: e.engine_nop(), [("out", s) for s in range(NSUB)], ())
    P.emit(sems, dsems)
    es.close()
    return nc


_CACHE = {}


def kernel(**inputs):
    x = np.ascontiguousarray(inputs["x"], dtype=np.float32)
    B, T, _ = x.shape
    if T not in _CACHE:
        _CACHE[T] = build(T)
    nc = _CACHE[T]
    shared = {}
    for k, v in inputs.items():
        if k == "x":
            continue
        a = np.ascontiguousarray(v, dtype=np.float32)
        if k != "norm_f_w":
            a = a[0]
        shared[k] = np.ascontiguousarray(a)
    in_maps = []
    for b in range(B):
        m = dict(shared)
        m["x"] = x[b]
        in_maps.append(m)
    res = run_bass_kernel_spmd(nc, in_maps, core_ids=list(range(B)))
    return np.stack([np.asarray(r["out"], dtype=np.float32) for r in res.results], axis=0)
```

```python
import numpy as np
from contextlib import ExitStack
import concourse.bass as bass
import concourse.mybir as mybir
from concourse.bass_utils import run_bass_kernel_spmd

F32 = mybir.dt.float32
BF16 = mybir.dt.bfloat16
AF = mybir.ActivationFunctionType
ALU = mybir.AluOpType

D = 1024
NH = 8
INC = 8208
FH = 2816
EPS = 1e-6
BIG = 30000.0


class Op:
    __slots__ = ("eng", "fn", "deps", "flag", "val", "dsem", "dval", "idx")


class Prog:
    def __init__(self, nc):
        self.nc = nc
        self.ops = []
        self.last_w = {}
        self.readers = {}
        self.dma_tot = {}
        self.dma_serial = {}

    def add(self, eng, fn, reads=(), writes=(), dma=None, serial=True):
        op = Op()
        op.eng = eng
        op.fn = fn
        op.flag = False
        op.val = None
        op.dsem = dma
        op.dval = None
        op.idx = len(self.ops)
        psr = [k for k in reads if isinstance(k, tuple) and k[0] == "ps"]
        if psr:
            reads = [k for k in reads if not (isinstance(k, tuple) and k[0] == "ps")]
            writes = list(writes) + psr
        deps = {}
        for k in reads:
            w = self.last_w.get(k)
            if w is not None:
                deps[w.idx] = w
        for k in writes:
            w = self.last_w.get(k)
            if w is not None:
                deps[w.idx] = w
            for r in self.readers.get(k, {}).values():
                deps[r.idx] = r
        for k in reads:
            self.readers.setdefault(k, {})[(eng, dma)] = op
        for k in writes:
            self.last_w[k] = op
            self.readers[k] = {}
        out = []
        for d in deps.values():
            if d.dsem is None and d.eng == "pe" and eng == "pe" and dma is None:
                continue
            ov = None
            if d.dsem is not None and not self.dma_serial[d.dsem]:
                ov = self.dma_tot[d.dsem]
            out.append((d, ov))
        op.deps = out
        if dma is not None:
            self.dma_tot[dma] = self.dma_tot.get(dma, 0) + 16
            self.dma_serial[dma] = serial
            op.dval = self.dma_tot[dma]
        self.ops.append(op)
        return op

    def emit(self, sems, dsems):
        nc = self.nc
        for op in self.ops:
            for d, _ in op.deps:
                d.flag = True
        cnt = {}
        for op in self.ops:
            if op.dsem is None and op.flag:
                cnt[op.eng] = cnt.get(op.eng, 0) + 1
                op.val = cnt[op.eng]
        self.max_sem = dict(cnt)
        per = {}
        for op in self.ops:
            per.setdefault(op.eng, []).append(op)
        engmap = {"pe": "tensor", "act": "scalar", "dve": "vector", "pool": "gpsimd", "sp": "sync"}
        with nc.Block() as block:
            for en, lst in per.items():
                def body(e, lst=lst):
                    waited = {}
                    for op in lst:
                        need = {}
                        for d, ov in op.deps:
                            if d.dsem is not None:
                                s = dsems[d.dsem]
                                v = d.dval if ov is None else ov
                                key = ("d", d.dsem)
                            else:
                                s = sems[d.eng]
                                v = d.val
                                key = ("e", d.eng)
                            if waited.get(key, 0) >= v:
                                continue
                            if key not in need or need[key][1] < v:
                                need[key] = (s, v)
                        for key, (s, v) in need.items():
                            e.wait_ge(s, v)
                            waited[key] = v
                        ins = op.fn(e)
                        if op.dsem is not None:
                            ins.then_inc(dsems[op.dsem], 16)
                        elif op.flag:
                            ins.then_inc(sems[op.eng], 1)
                getattr(block, engmap[en])(body)


R_N1, R_BGLU, R_BGAT, R_DWB, R_LNW, R_LNB, R_BPW2, R_N2, R_DW, R_DNC = 0, 1, 3, 5, 6, 7, 8, 9, 10, 41
NROWS = 53


class _Stop(Exception):
    pass


def build(T, NSUB=4, dbg=False, limit=None):
    def stage(n):
        if limit is not None and n >= limit:
            raise _Stop()
    try:
        return _build(T, NSUB, stage)
    except _Stop:
        raise RuntimeError("unreachable")


def _build(T, NSUB, stage):
    import os
    SKIP = os.environ.get("SKIP", "")
    TT = 128 * NSUB
    NT = T // TT
    assert NT * TT == T
    nc = bass.Bass("TRN2", target_bir_lowering=False)
    P = Prog(nc)
    es = ExitStack()

    def din(name, shape):
        return nc.dram_tensor(name, list(shape), F32, kind="ExternalInput").ap()

    x = din("x", [T, D])
    norm1_w = din("norm1_w", [D])
    w_in = din("w_in", [D, INC])
    b_glu = din("b_glu", [2 * D])
    b_gates = din("b_gates", [2 * D])
    dn_conv_w = din("dn_conv_w", [4, 3 * D])
    dn_A_log = din("dn_A_log", [NH])
    dn_dt_bias = din("dn_dt_bias", [NH])
    dn_norm_w = din("dn_norm_w", [128])
    dn_w_o = din("dn_w_o", [D, D])
    cm_dw_w = din("cm_dw_w", [31, D])
    cm_dw_b = din("cm_dw_b", [D])
    cm_ln_w = din("cm_ln_w", [D])
    cm_ln_b = din("cm_ln_b", [D])
    cm_w_pw2 = din("cm_w_pw2", [D, D])
    cm_b_pw2 = din("cm_b_pw2", [D])
    w_out = din("w_out", [D, D])
    norm2_w = din("norm2_w", [D])
    ffn_gu = din("ffn_w_gate_up", [D, 2 * FH])
    ffn_dn = din("ffn_w_down", [FH, D])
    norm_f_w = din("norm_f_w", [D])
    out = nc.dram_tensor("out", [T, D], F32, kind="ExternalOutput").ap()

    def dscr(name, shape):
        return nc.dram_tensor(name, list(shape), BF16, kind="Internal").ap()

    wb_in = dscr("wb_in", [D, INC])
    wb_o = dscr("wb_o", [D, D])
    wb_pw2 = dscr("wb_pw2", [D, D])
    wb_out = dscr("wb_out", [D, D])
    wb_gu = dscr("wb_gu", [D, 2 * FH])
    wb_dn = dscr("wb_dn", [FH, D])

    def sb(name, shape, dt=F32):
        return es.enter_context(nc.sbuf_tensor(name, list(shape), dt))

    ident_b = sb("ident_b", [128, 128], BF16)
    ones_b = sb("ones_b", [128, 128], BF16)
    ones_f = sb("ones_f", [128, 128])
    idf = sb("idf", [128, 128])
    tri_f = sb("tri_f", [128, 128])
    chk_f = sb("chk_f", [128, 128])
    neg_t = sb("neg_t", [128, 128])
    pos_s = sb("pos_s", [128, 128])
    par = sb("par", [128, 8, NROWS])
    dnw = sb("dnw", [128, 1])
    mhalf = sb("mhalf", [128, 1])
    dtb = sb("dtb", [128, NH])
    nega = sb("nega", [128, NH])
    nfw = sb("nfw", [128, D])
    wbd = sb("wbd", [128, 8, 16], BF16)

    XIN = [sb("xin%d" % s, [128, D]) for s in range(NSUB)]
    XS = [sb("xs%d" % s, [128, D]) for s in range(NSUB)]
    HB = [sb("hb%d" % i, [128, D], BF16) for i in range(2)]
    HT = sb("ht", [128, 8, TT], BF16)
    STT_ = sb("stt", [128, 8])
    PRE = [sb("pre%d" % i, [128, TT + 4], BF16) for i in range(2)]
    HALO = sb("halo", [128, 24, 4], BF16)
    NDG = 12
    DG = [sb("dg%d" % i, [128, 128], BF16) for i in range(NDG)]
    QKV = sb("qkv", [128, 24, TT], BF16)
    QN = QKV[:, 0:8, :]
    KN = QKV[:, 8:16, :]
    VS = QKV[:, 16:24, :]
    FF = QKV[:, 0:22, :]
    ZS = sb("zs", [128, 8, TT], BF16)
    CIN = sb("cin", [128, 8, 30 + TT], BF16)
    CN = CIN[:, :, 30:30 + TT]
    OG = sb("og", [128, 8, TT], BF16)
    MRG = ZS
    MA = [sb("ma%d" % i, [128, TT]) for i in range(2)]
    TMPW = [sb("tmpw%d" % i, [128, TT]) for i in range(2)]
    SQW = [sb("sqw%d" % i, [128, TT], BF16) for i in range(2)]
    SQL = [sb("sql%d" % i, [128, TT], BF16) for i in range(2)]
    SM = sb("sm", [128, NSUB, 12, NH])
    DSC = sb("dsc", [128, 4096])
    dsc3 = lambda i: DSC[:, i * 1024:(i + 1) * 1024].rearrange("p (h i) -> p h i", h=8)
    TRIG, EGB, XX, XA = dsc3(0), dsc3(1), dsc3(2), dsc3(3)
    CCV = DSC[:, 0:8 * TT].rearrange("p (c t) -> p c t", c=8)
    GATES = DSC[:, :].bitcast(BF16)[:, 0:16 * TT].rearrange("p (c t) -> p c t", c=16)
    RKN = ["trig", "egb", "xx", "xa"]

    def rk_ccv(c):
        names = sorted({RKN[(c * TT) // 1024], RKN[((c + 1) * TT - 1) // 1024]})
        return [(n, hs) for n in names for hs in (0, 1)]

    def rk_gates(c):
        return [(RKN[(c * TT // 2) // 1024], hs) for hs in (0, 1)]
    XB = sb("xb", [128, 8, 128])
    MMc = [sb("mm%d" % i, [128, 8, 128], BF16) for i in range(2)]
    NNc = [sb("nn%d" % i, [128, 8, 128], BF16) for i in range(2)]
    PF = sb("pf", [128, 8, 128])
    pfb = PF[:, :, :].rearrange("p h i -> p (h i)").bitcast(BF16)
    PBF = sb("pbf", [128, 8, 128], BF16)
    KBG = pfb[:, 0:1024].rearrange("p (h i) -> p h i", h=8)
    KD = pfb[:, 1024:2048].rearrange("p (h i) -> p h i", h=8)
    VB = sb("vb", [128, 8, 128], BF16)
    WT = sb("wt", [128, 8, 128], BF16)
    U = XB
    ATT = sb("att", [128, 8, 128], BF16)
    QD = sb("qd", [128, 8, 128], BF16)
    VN = sb("vn", [128, 8, 128], BF16)
    SQ = DSC[:, 0:512].bitcast(BF16).rearrange("p (h i) -> p h i", h=8)
    RR = TRIG
    T1 = WT
    S = sb("s", [128, 8, 128])
    SBF = sb("sbf", [128, 8, 128], BF16)
    NSLOT = 4
    WR = [sb("wr%d" % i, [128, 8, 512], BF16) for i in range(NSLOT)]
    rows = WR[3][:, :, :].rearrange("p k c -> p (k c)").bitcast(F32)[0:64, 0:D]

    PW = [es.enter_context(nc.psum_tensor("pw%d" % j, [128, 1024], F32)) for j in range(4)]

    def bank(b):
        return PW[b // 2][:, (b % 2) * 512:(b % 2) * 512 + 512]

    def bk(b):
        return bank(b)[:, 0:TT]

    def pw3(j):
        return PW[j][:, :].rearrange("p (h i) -> p h i", h=8)

    def pwb(j):
        return PW[j][:, 0:512].bitcast(BF16).rearrange("p (h i) -> p h i", h=8)

    sems = {k: es.enter_context(nc.semaphore("s_" + k)) for k in ("pe", "act", "dve", "pool")}
    dnames = ["setup", "c_in", "c_o", "c_pw2", "c_out", "c_gu", "c_dn"] + \
        ["w%d" % i for i in range(NSLOT)] + ["xl%d" % s for s in range(NSUB)] + ["st%d" % s for s in range(NSUB)]
    dsems = {k: es.enter_context(nc.semaphore("d_" + k)) for k in dnames}

    def MM(o, l, r, start, stop, rd, wr):
        P.add("pe", lambda e: e.matmul(out=o, lhsT=l, rhs=r, start=start, stop=stop), rd, wr)

    def TR(o, i, idn, rd, wr):
        P.add("pe", lambda e: e.transpose(out=o, in_=i, identity=idn), rd, wr)

    def ACT(o, i, func, rd, wr, bias=None, scale=None, accum=None):
        kw = {}
        if bias is not None:
            kw["bias"] = bias
        if scale is not None:
            kw["scale"] = scale
        if accum is not None:
            kw["accum_out"] = accum
        P.add("act", lambda e: e.activation(out=o, in_=i, func=func, **kw), rd, wr)

    def TS(eng, o, i, s1, s2, op0, op1, rd, wr):
        if op1 is None:
            P.add(eng, lambda e: e.tensor_scalar(out=o, in0=i, scalar1=s1, scalar2=None, op0=op0), rd, wr)
        else:
            P.add(eng, lambda e: e.tensor_scalar(out=o, in0=i, scalar1=s1, scalar2=s2, op0=op0, op1=op1), rd, wr)

    def TTo(eng, o, a, b, op, rd, wr):
        P.add(eng, lambda e: e.tensor_tensor(out=o, in0=a, in1=b, op=op), rd, wr)

    def STT(o, a, sc, b, op0, op1, rd, wr):
        P.add("dve", lambda e: e.scalar_tensor_tensor(out=o, in0=a, scalar=sc, in1=b, op0=op0, op1=op1), rd, wr)

    def CP(eng, o, i, rd, wr):
        if eng == "act":
            P.add("act", lambda e: e.activation(out=o, in_=i, func=AF.Copy), rd, wr)
        else:
            P.add(eng, lambda e: e.tensor_copy(out=o, in_=i), rd, wr)

    def MSET(eng, o, v, wr):
        P.add(eng, lambda e: e.memset(o, v), (), wr)

    def DMA(eng, o, i, rd, wr, sem, serial=True, slow=False):
        if slow:
            P.add(eng, lambda e: e.dma_start(out=o, in_=i, allow_slow_non_contiguous=True), rd, wr, dma=sem, serial=serial)
        else:
            P.add(eng, lambda e: e.dma_start(out=o, in_=i), rd, wr, dma=sem, serial=serial)

    def cast(src, dst, nrows, sem, key, rstep=128):
        if "cast" in SKIP:
            return []
        for r0 in range(0, nrows, rstep):
            r1 = min(nrows, r0 + rstep)
            DMA("pool", dst[r0:r1, :], src[r0:r1, :], (), [(key, r0)], sem, serial=False)
        return [(key, r0) for r0 in range(0, nrows, rstep)]

    K_in = cast(w_in, wb_in, D, "c_in", "wb_in")
    for s in range(NSUB):
        DMA("sp", XIN[s][:, :], x[s * 128:(s + 1) * 128, :], (), [("xin", s)], "xl%d" % s)
    MSET("dve", rows[:, :], 0.0, [("wr", 3)])
    prow = [(norm1_w, R_N1, 1), (b_glu, R_BGLU, 2), (b_gates, R_BGAT, 2), (cm_dw_b, R_DWB, 1), (cm_ln_w, R_LNW, 1),
            (cm_ln_b, R_LNB, 1), (cm_b_pw2, R_BPW2, 1), (norm2_w, R_N2, 1)]
    for ap_, r0, n in prow:
        DMA("sp", rows[r0:r0 + n, :], ap_.rearrange("(r c) -> r c", r=n), (), [("wr", 3)], "setup", serial=False)
    DMA("sp", rows[R_DW:R_DW + 31, :], cm_dw_w[:, :], (), [("wr", 3)], "setup", serial=False)
    for k in range(4):
        DMA("sp", rows[R_DNC + 3 * k:R_DNC + 3 * k + 3, :], dn_conv_w[k, :].rearrange("(r c) -> r c", r=3), (), [("wr", 3)],
            "setup", serial=False)
    DMA("sp", dnw[:, :], dn_norm_w.rearrange("(p o) -> p o", o=1), (), ["dnw"], "setup", serial=False)
    DMA("sp", dtb[:, :], dn_dt_bias.partition_broadcast(128), (), ["dtb"], "setup", serial=False)
    DMA("sp", nega[:, :], dn_A_log.partition_broadcast(128), (), ["nega"], "setup", serial=False)
    DMA("sp", nfw[:, :], norm_f_w.partition_broadcast(128), (), ["nfw"], "setup", serial=False)
    K_o = cast(dn_w_o, wb_o, D, "c_o", "wb_o", 256)
    K_pw2 = cast(cm_w_pw2, wb_pw2, D, "c_pw2", "wb_pw2", 256)
    K_out = cast(w_out, wb_out, D, "c_out", "wb_out", 256)
    K_gu = cast(ffn_gu, wb_gu, D, "c_gu", "wb_gu")
    K_dn = cast(ffn_dn, wb_dn, FH, "c_dn", "wb_dn", 256)
    DMA("sp", wbd[:, :, :], wb_in[:, 4096:4112].rearrange("(k p) c -> p k c", p=128), K_in, ["wbd"], "setup",
        serial=False, slow=True)

    MSET("pool", ones_f[:, :], 1.0, ["ones_f"])
    MSET("pool", mhalf[:, :], -0.5, ["mhalf"])
    MSET("pool", ones_b[:, :], 1.0, ["ones_b"])
    MSET("pool", idf[:, :], 0.0, ["idf"])
    P.add("pool", lambda e: e.affine_select(out=idf[:, :], in_=idf[:, :], pattern=[[-1, 128]], compare_op=ALU.not_equal,
                                            fill=1.0, base=0, channel_multiplier=1), ["idf"], ["idf"])
    CP("pool", ident_b[:, :], idf[:, :], ["idf"], ["ident_b"])
    P.add("pool", lambda e: e.affine_select(out=tri_f[:, :], in_=ones_f[:, :], pattern=[[1, 128]], compare_op=ALU.is_ge,
                                            fill=0.0, base=0, channel_multiplier=-1), ["ones_f"], ["tri_f"])
    MSET("pool", tri_f[0:64, 64:128], 0.0, ["tri_f"])
    CP("pool", chk_f[:, :], ones_f[:, :], ["ones_f"], ["chk_f"])
    MSET("pool", chk_f[0:64, 64:128], 0.0, ["chk_f"])
    MSET("pool", chk_f[64:128, 0:64], 0.0, ["chk_f"])
    TS("pool", neg_t[:, :], tri_f[:, :], -1.0, BIG, ALU.add, ALU.mult, ["tri_f"], ["neg_t"])
    P.add("pool", lambda e: e.affine_select(out=pos_s[:, :], in_=ones_f[:, :], pattern=[[-1, 128]], compare_op=ALU.is_gt,
                                            fill=0.0, base=0, channel_multiplier=1), ["ones_f"], ["pos_s"])
    MSET("pool", pos_s[64:128, 0:64], 0.0, ["pos_s"])
    TS("pool", pos_s[:, :], pos_s[:, :], -BIG, BIG, ALU.mult, ALU.add, ["pos_s"], ["pos_s"])
    MSET("pool", HALO[:, :, :], 0.0, ["halo"])
    MSET("pool", CIN[:, :, 0:30], 0.0, [("cin", c) for c in range(8)])
    MSET("pool", S[:, :, :], 0.0, [("S", h) for h in range(NH)])
    MSET("pool", SBF[:, :, :], 0.0, [("SBF", h) for h in range(NH)])
    for c in range(8):
        TR(bank(0)[:, c * 64:c * 64 + NROWS], rows[0:NROWS, c * 128:(c + 1) * 128], idf[0:NROWS, 0:NROWS],
           [("wr", 3), "idf"], [("ps", 0)])
    CP("dve", par[:, :, :], bank(0).rearrange("p (c r) -> p c r", c=8)[:, :, 0:NROWS], [("ps", 0)], ["par"])
    ACT(nega[:, :], nega[:, :], AF.Exp, ["nega"], ["nega"])
    TS("dve", nega[:, :], nega[:, :], -1.0, None, ALU.mult, None, ["nega"], ["nega"])

    loads = []

    def plan_tile():
        L = []
        for g in range(6):
            L.append((wb_in, 0, 8, g * 512, 512, K_in))
        for g in range(2):
            L.append((wb_in, 0, 8, 4112 + g * 512, 512, K_in))
            L.append((wb_in, 0, 8, 4112 + 1024 + g * 512, 512, K_in))
        for g in range(2):
            L.append((wb_in, 0, 8, 3072 + g * 512, 512, K_in))
        for g in range(4):
            L.append((wb_in, 0, 8, 6160 + g * 512, 512, K_in))
        for g in range(2):
            L.append((wb_o, 0, 8, g * 512, 512, K_o))
        for g in range(2):
            L.append((wb_pw2, 0, 8, g * 512, 512, K_pw2))
        for g in range(2):
            L.append((wb_out, 0, 8, g * 512, 512, K_out))
        for g in range(6):
            nc_ = 512 if g < 5 else 256
            L.append((wb_gu, 0, 8, g * 512, nc_, K_gu))
            L.append((wb_gu, 0, 8, FH + g * 512, nc_, K_gu))
        for hf in range(2):
            for kg in range(3):
                nk = 8 if kg < 2 else 6
                L.append((wb_dn, kg * 1024, nk, hf * 512, 512, K_dn))
        return L

    for t in range(NT):
        loads.extend(plan_tile())
    issued = [0]
    consumed = [0]

    def issue_to(n):
        while issued[0] < min(n, len(loads)):
            m = issued[0]
            dr, r0, nk, c0, ncol, ck = loads[m]
            slot = m % NSLOT
            src = dr[r0:r0 + 128 * nk, c0:c0 + ncol].rearrange("(k p) c -> p k c", p=128)
            DMA("sp", WR[slot][:, 0:nk, 0:ncol], src, ck, [("wr", slot)], "w%d" % slot)
            issued[0] += 1

    def next_w():
        n = consumed[0]
        consumed[0] += 1
        assert issued[0] >= n + 1 or n < NSLOT + 100000
        issue_to(n + 1)
        return n % NSLOT

    def fin():
        issue_to(consumed[0] + NSLOT)

    dgc = [0]
    DG_ENG = ("dve", "pool")

    def build_diag(sc):
        i = dgc[0] % NDG
        eng = DG_ENG[dgc[0] % len(DG_ENG)]
        dgc[0] += 1
        if eng == "act":
            ACT(DG[i][:, :], ident_b[:, :], AF.Identity, ["ident_b", "par"], [("dg", i)], scale=sc)
        else:
            TS(eng, DG[i][:, :], ident_b[:, :], sc, 0.0, ALU.mult, ALU.add, ["ident_b", "par"], [("dg", i)])
        return DG[i][:, :], ("dg", i)

    def tok_rstd(src, junk, jkey, col, rd, scale_n):
        ACT(junk, src, AF.Square, rd, [jkey, ("st", col)], accum=STT_[:, col:col + 1])
        TS("dve", STT_[:, col:col + 1], STT_[:, col:col + 1], scale_n, EPS, ALU.mult, ALU.add, [("st", col)], [("st", col)])
        TTo("pool", STT_[:, col:col + 1], STT_[:, col:col + 1], mhalf[:, 0:1], ALU.pow, [("st", col), "mhalf"], [("st", col)])

    def norm_stats(s, src, skey):
        tok_rstd(src[:, :], HB[s % 2][:, :], ("hb", s % 2), s, [skey], 1.0 / D)

    def norm_apply(s, src, skey, nrow, pb0):
        hb = HB[s % 2]
        TS("dve", hb[:, :], src[:, :], STT_[:, s:s + 1], None, ALU.mult, None, [skey, ("st", s)], [("hb", s % 2)])
        for c in range(8):
            TR(bank(pb0 + c // 4).bitcast(BF16)[:, (c % 4) * 128:(c % 4) * 128 + 128],
               hb[:, c * 128:(c + 1) * 128], ident_b[:, :], [("hb", s % 2), "ident_b"], [("ps", pb0 + c // 4)])
        for hf in range(2):
            srcp = bank(pb0 + hf).bitcast(BF16)[:, 0:512].rearrange("p (c i) -> p c i", c=4)
            TTo("dve", HT[:, hf * 4:hf * 4 + 4, s * 128:(s + 1) * 128], srcp,
                par[:, hf * 4:hf * 4 + 4, nrow:nrow + 1].to_broadcast([128, 4, 128]), ALU.mult,
                [("ps", pb0 + hf), "par"], [("ht", s)])

    HTK = [("ht", s) for s in range(NSUB)]
    subs = lambda s: slice(s * 128, (s + 1) * 128)

    try:
      stage(0)
      for t in range(NT):
        tok0 = t * TT
        if t == 0:
            for s in range(NSUB):
                norm_stats(s, XIN[s], ("xin", s))
            for s in range(NSUB):
                norm_apply(s, XIN[s], ("xin", s), R_N1, 0)

        stage(1)
        for s in range(NSUB):
            for kc in range(8):
                MM(bank(2)[:, s * 16:s * 16 + 16], HT[:, kc, subs(s)], wbd[:, kc, :], kc == 0, kc == 7,
                   [("ht", s), "wbd"], [("ps", 2)])
        bdp = bank(2)[:, 0:NSUB * 16].rearrange("p (s c) -> p s c", s=NSUB)
        ACT(SM[:, :, 0, :], bdp[:, :, 0:8], AF.Sigmoid, [("ps", 2)], ["sm"])
        TTo("dve", SM[:, :, 1, :], bdp[:, :, 8:16], dtb[:, :].unsqueeze(1).to_broadcast([128, NSUB, NH]), ALU.add,
            [("ps", 2), "dtb"], ["sm"])


        stage(2)
        def qkv_post(ch):
            seg, hh = ch // 8, ch % 8
            pbk = ch % 2
            pre = PRE[ch % 2]
            CP("pool", pre[:, 0:4], HALO[:, ch, :], ["halo"], [("pre", ch % 2)])
            ACT(pre[:, 4:4 + TT], bk(pbk), AF.Identity, [("ps", pbk)], [("pre", ch % 2)])
            CP("pool", HALO[:, ch, :], pre[:, TT:TT + 4], [("pre", ch % 2)], ["halo"])
            for k in range(4):
                dg, dk_ = build_diag(par[:, ch % 8, R_DNC + 3 * k + seg:R_DNC + 3 * k + seg + 1])
                MM(bk(2 + pbk), dg, pre[:, 1 + k:1 + k + TT], k == 0, k == 3, [dk_, ("pre", ch % 2)], [("ps", 2 + pbk)])
            dst = (QN, KN, VS)[seg]
            ACT(dst[:, hh, :], bk(2 + pbk), AF.Silu, [("ps", 2 + pbk)], [("qkv", seg, hh)])

        for g in range(6):
            slot = next_w()
            for j in range(4):
                ch = g * 4 + j
                pbk = ch % 2
                for kc in range(8):
                    MM(bk(pbk), WR[slot][:, kc, j * 128:(j + 1) * 128], HT[:, kc, :], kc == 0, kc == 7,
                       HTK + [("wr", slot)], [("ps", pbk)])
                if ch >= 1:
                    qkv_post(ch - 1)
            fin()
        qkv_post(23)
        stage(3)
        stage(4)
        for g in range(2):
            sa = next_w()
            sbk = next_w()
            for j in range(4):
                c = g * 4 + j
                ba, bb = 2 * (c % 2), 2 * (c % 2) + 1
                for kc in range(8):
                    MM(bk(ba), WR[sa][:, kc, j * 128:(j + 1) * 128], HT[:, kc, :], kc == 0, kc == 7,
                       HTK + [("wr", sa)], [("ps", ba)])
                for kc in range(8):
                    MM(bk(bb), WR[sbk][:, kc, j * 128:(j + 1) * 128], HT[:, kc, :], kc == 0, kc == 7,
                       HTK + [("wr", sbk)], [("ps", bb)])
                tw = TMPW[c % 2]
                ACT(tw[:, :], bk(bb), AF.Sigmoid, [("ps", bb), "par"], [("tmpw", c % 2)], bias=par[:, c, R_BGLU + 1:R_BGLU + 2])
                STT(CIN[:, c, 30:30 + TT], bk(ba), par[:, c, R_BGLU:R_BGLU + 1], tw[:, :], ALU.add, ALU.mult,
                    [("ps", ba), "par", ("tmpw", c % 2)], [("cin", c)])
            fin()
        stage(7)
        ACT(SM[:, :, 2, :], SM[:, :, 1, :], AF.Exp, ["sm"], ["sm"])
        ACT(SM[:, :, 3, :], SM[:, :, 2, :], AF.Ln, ["sm"], ["sm"], bias=1.0)
        TTo("dve", SM[:, :, 4, :], SM[:, :, 3, :], nega[:, :].unsqueeze(1).to_broadcast([128, NSUB, NH]), ALU.mult,
            ["sm", "nega"], ["sm"])
        for s in range(NSUB):
            MM(bank(2)[:, 64 + s * 16:64 + s * 16 + 8], tri_f[:, :], SM[:, s, 4, :], True, True, ["tri_f", "sm"], [("ps", 2)])
            MM(bank(2)[:, 64 + s * 16 + 8:64 + s * 16 + 16], chk_f[:, :], SM[:, s, 4, :], True, True, ["chk_f", "sm"], [("ps", 2)])
        gcp = bank(2)[:, 64:64 + NSUB * 16].rearrange("p (s c) -> p s c", s=NSUB)
        CP("dve", SM[:, :, 5, :], gcp[:, :, 0:8], [("ps", 2)], ["sm"])
        CP("dve", SM[:, :, 6, :], gcp[:, :, 8:16], [("ps", 2)], ["sm"])
        ACT(SM[:, :, 7, :], SM[:, :, 5, :], AF.Exp, ["sm"], ["sm"])
        TTo("dve", SM[:, :, 8, :], SM[:, :, 0, :], SM[:, :, 7, :], ALU.mult, ["sm"], ["sm"])
        TTo("dve", SM[:, :, 9, :], SM[:, :, 6, :], SM[:, :, 5, :], ALU.subtract, ["sm"], ["sm"])
        ACT(SM[:, :, 10, :], SM[:, :, 9, :], AF.Exp, ["sm"], ["sm"])

        stage(8)
        def l2_sq(seg, buf, hh, i2):
            TTo("dve", SQW[i2][:, :], buf[:, hh, :], buf[:, hh, :], ALU.mult, [("qkv", seg, hh)], [("sqw", i2)])

        def l2_mm(i2):
            MM(bk(4 + i2), ones_b[:, :], SQW[i2][:, :], True, True, [("sqw", i2), "ones_b"], [("ps", 4 + i2)])
            ACT(TMPW[i2][:, :], bk(4 + i2), AF.Ln, [("ps", 4 + i2)], [("tmpw", i2)], bias=EPS)
            ACT(TMPW[i2][:, :], TMPW[i2][:, :], AF.Exp, [("tmpw", i2)], [("tmpw", i2)], scale=-0.5)

        def l2_mul(seg, buf, hh, i2):
            TTo("dve", buf[:, hh, :], buf[:, hh, :], TMPW[i2][:, :], ALU.mult, [("qkv", seg, hh), ("tmpw", i2)],
                [("qkv", seg, hh)])

        def ln_cast(c):
            ACT(SQL[0][:, :], CCV[:, c, :], AF.Copy, [*rk_ccv(c)], [("sql", 0)])
            ACT(SQL[1][:, :], CCV[:, c, :], AF.Square, [*rk_ccv(c)], [("sql", 1)])

        def ln_mm(c, which):
            MM(bk(6 + which), ones_b[:, :], SQL[which][:, :], c == 0, c == 7, [("sql", which), "ones_b"], [("ps", 6 + which)])

        for c in range(8):
            pbk = c % 2
            for k in range(31):
                dg, dk_ = build_diag(par[:, c, R_DW + k:R_DW + k + 1])
                MM(bk(pbk), dg, CIN[:, c, k:k + TT], k == 0, k == 30, [dk_, ("cin", c)], [("ps", pbk)])
                if k == 2:
                    l2_sq(0, QN, c, 0)
                elif k == 4:
                    l2_sq(1, KN, c, 1)
                elif k == 6 and c >= 1:
                    ln_cast(c - 1)
                elif k == 10:
                    l2_mm(0)
                elif k == 14:
                    l2_mm(1)
                elif k == 17 and c >= 1:
                    ln_mm(c - 1, 0)
                elif k == 20 and c >= 1:
                    ln_mm(c - 1, 1)
                elif k == 24:
                    l2_mul(0, QN, c, 0)
                elif k == 28:
                    l2_mul(1, KN, c, 1)
            ACT(CCV[:, c, :], bk(pbk), AF.Identity, [("ps", pbk), "par"], [*rk_ccv(c)], bias=par[:, c, R_DWB:R_DWB + 1])
            CP("pool", CIN[:, c, 0:30], CIN[:, c, TT:TT + 30], [("cin", c)], [("cin", c)])
        ln_cast(7)
        ln_mm(7, 0)
        ln_mm(7, 1)
        stage(9)
        TS("dve", MA[0][:, :], bk(6), 1.0 / D, None, ALU.mult, None, [("ps", 6)], [("ma", 0)])
        TTo("dve", TMPW[0][:, :], MA[0][:, :], MA[0][:, :], ALU.mult, [("ma", 0)], [("tmpw", 0)])
        STT(MA[1][:, :], bk(7), 1.0 / D, TMPW[0][:, :], ALU.mult, ALU.subtract, [("ps", 7), ("tmpw", 0)], [("ma", 1)])
        ACT(MA[1][:, :], MA[1][:, :], AF.Ln, [("ma", 1)], [("ma", 1)], bias=EPS)
        ACT(MA[1][:, :], MA[1][:, :], AF.Exp, [("ma", 1)], [("ma", 1)], scale=-0.5)
        for g in range(2):
            slot = next_w()
            for j in range(4):
                c = g * 4 + j
                pbk = j % 2
                for kc in range(8):
                    MM(bk(pbk), WR[slot][:, kc, j * 128:(j + 1) * 128], HT[:, kc, :], kc == 0, kc == 7,
                       HTK + [("wr", slot)], [("ps", pbk)])
                ACT(ZS[:, c, :], bk(pbk), AF.Silu, [("ps", pbk)], [("zs", c)])
                TTo("dve", CCV[:, c, :], CCV[:, c, :], MA[0][:, :], ALU.subtract, [*rk_ccv(c), ("ma", 0)], [*rk_ccv(c)])
                TTo("pool" if c % 3 == 2 else "dve", CCV[:, c, :], CCV[:, c, :], MA[1][:, :], ALU.mult, [*rk_ccv(c), ("ma", 1)],
                    [*rk_ccv(c)])
                ACT(CN[:, c, :], CCV[:, c, :], AF.Silu, [*rk_ccv(c), "par"], [("cin", c)],
                    bias=par[:, c, R_LNB:R_LNB + 1], scale=par[:, c, R_LNW:R_LNW + 1])
            fin()

        stage(10)
        HS = (0, 1)

        def hsl(hs):
            return slice(4 * hs, 4 * hs + 4)

        def hrange(hs):
            return range(4 * hs, 4 * hs + 4)

        def pwbh(j, hs):
            return bank(2 * j + hs).bitcast(BF16)[:, 0:512].rearrange("p (h i) -> p h i", h=4)

        def bc4(v, hs):
            return v[:, hsl(hs)].unsqueeze(2).to_broadcast([128, 4, 128])

        def mb4(m):
            return m[:, :].unsqueeze(1).to_broadcast([128, 4, 128])

        def d_prep(s):
            g_, beta_, gc_ = SM[:, s, 4, :], SM[:, s, 0, :], SM[:, s, 5, :]
            for hs in HS:
                TTo("dve", TRIG[:, hsl(hs), :], mb4(tri_f), bc4(g_, hs), ALU.mult, ["tri_f", "sm"], [("trig", hs)])
            for hs in HS:
                for h in hrange(hs):
                    MM(pw3(1)[:, h, :], ones_f[:, :], TRIG[:, h, :], True, True, [("trig", hs), "ones_f"], [("ps", 2 + hs)])
            for hs in HS:
                ACT(EGB[:, hsl(hs), :], pw3(1)[:, hsl(hs), :], AF.Exp, [("ps", 2 + hs)], [("egb", hs)])
                TTo("dve", XX[:, hsl(hs), :], pw3(1)[:, hsl(hs), :], bc4(gc_, hs), ALU.subtract, [("ps", 2 + hs), "sm"], [("xx", hs)])
            for hs in HS:
                TTo("dve", XB[:, hsl(hs), :], XX[:, hsl(hs), :], mb4(pos_s), ALU.add, [("xx", hs), "pos_s"], [("xb", hs)])
            for hs in HS:
                ACT(XB[:, hsl(hs), :], XB[:, hsl(hs), :], AF.Exp, [("xb", hs)], [("xb", hs)], scale=-1.0)
            for hs in HS:
                TTo("dve", XB[:, hsl(hs), :], XB[:, hsl(hs), :], bc4(beta_, hs), ALU.mult, [("xb", hs), "sm"], [("xb", hs)])

        def d_head(s):
            sl = subs(s)
            stage(11)
            for hs in HS:
                for h in hrange(hs):
                    MM(pw3(2)[:, h, :], KN[:, h, sl], KN[:, h, sl], True, True, [("qkv", 1, h)], [("ps", 4 + hs)])
            for hs in HS:
                TTo("dve", MMc[0][:, hsl(hs), :], pw3(2)[:, hsl(hs), :], XB[:, hsl(hs), :], ALU.mult, [("ps", 4 + hs), ("xb", hs)],
                    [("mm", 0, hs)])
            for hs in HS:
                for h in hrange(hs):
                    TR(pwbh(3, hs)[:, h - 4 * hs, :], MMc[0][:, h, :], ident_b[:, :], [("mm", 0, hs), "ident_b"], [("ps", 6 + hs)])
            for hs in HS:
                CP("act", NNc[0][:, hsl(hs), :], pwbh(3, hs), [("ps", 6 + hs)], [("nn", 0, hs)])
                STT(PBF[:, hsl(hs), :], pwbh(3, hs), -1.0, mb4(idf), ALU.mult, ALU.add, [("ps", 6 + hs), "idf"], [("pbf", hs)])

        def d_tail(s):
            sl = subs(s)
            beta_ = SM[:, s, 0, :]
            stage(12)
            for hs in HS:
                TTo("pool", XA[:, hsl(hs), :], XX[:, hsl(hs), :], mb4(neg_t), ALU.add, [("xx", hs), "neg_t"], [("xa", hs)])
                ACT(XA[:, hsl(hs), :], XA[:, hsl(hs), :], AF.Exp, [("xa", hs)], [("xa", hs)])
            for l in range(1, 6):
                cur, nxt = (l - 1) % 2, l % 2
                if l == 2:
                    for hs in HS:
                        STT(QD[:, hsl(hs), :], QN[:, hsl(hs), sl], 128.0 ** -0.5, EGB[:, hsl(hs), :], ALU.mult, ALU.mult,
                            [("qkv", 0, h) for h in hrange(hs)] + [("egb", hs)], [("qd", hs)])
                if l == 5:
                    for hs in HS:
                        for h in hrange(hs):
                            TR(pwbh(2, hs)[:, h - 4 * hs, :], VS[:, h, sl], ident_b[:, :], [("qkv", 2, h), "ident_b"], [("ps", 4 + hs)])
                    for hs in HS:
                        TTo("dve", VB[:, hsl(hs), :], pwbh(2, hs), bc4(beta_, hs), ALU.mult, [("ps", 4 + hs), "sm"], [("vb", hs)])
                for hs in HS:
                    for h in hrange(hs):
                        MM(pw3(1)[:, h, :], NNc[cur][:, h, :], MMc[cur][:, h, :], True, True, [("nn", cur, hs), ("mm", cur, hs)],
                           [("ps", 2 + hs)])
                    if l < 5:
                        for h in hrange(hs):
                            MM(pw3(2)[:, h, :], MMc[cur][:, h, :], NNc[cur][:, h, :], True, True, [("nn", cur, hs), ("mm", cur, hs)],
                               [("ps", 4 + hs)])
                for hs in HS:
                    CP("act", MMc[nxt][:, hsl(hs), :], pw3(1)[:, hsl(hs), :], [("ps", 2 + hs)], [("mm", nxt, hs)])
                    if l < 5:
                        CP("act" if hs == 0 else "dve", NNc[nxt][:, hsl(hs), :], pw3(2)[:, hsl(hs), :], [("ps", 4 + hs)], [("nn", nxt, hs)])
                for hs in HS:
                    for h in hrange(hs):
                        MM(pw3(3)[:, h, :], MMc[nxt][:, h, :], PBF[:, h, :], True, False, [("mm", nxt, hs), ("pbf", hs)], [("ps", 6 + hs)])
                        MM(pw3(3)[:, h, :], ident_b[:, :], PBF[:, h, :], False, True, ["ident_b", ("pbf", hs)], [("ps", 6 + hs)])
                for hs in HS:
                    CP("dve", PBF[:, hsl(hs), :], pw3(3)[:, hsl(hs), :], [("ps", 6 + hs)], [("pbf", hs)])
                if l == 4 and pending_onorm[0] is not None:
                    d_onorm(pending_onorm[0])
                    pending_onorm[0] = None
            stage(13)
            for hs in HS:
                for h in hrange(hs):
                    TR(pwbh(1, hs)[:, h - 4 * hs, :], KN[:, h, sl], ident_b[:, :], [("qkv", 1, h), "ident_b"], [("ps", 2 + hs)])
            for hs in HS:
                TTo("dve", KBG[:, hsl(hs), :], pwbh(1, hs), bc4(SM[:, s, 8, :], hs), ALU.mult, [("ps", 2 + hs), "sm"], [("pf", hs)])
                TTo("dve", KD[:, hsl(hs), :], pwbh(1, hs), bc4(SM[:, s, 10, :], hs), ALU.mult, [("ps", 2 + hs), "sm"], [("pf", hs)])
            stage(14)
            for hs in HS:
                for h in hrange(hs):
                    MM(pw3(3)[:, h, :], KBG[:, h, :], PBF[:, h, :], True, True, [("pf", hs), ("pbf", hs)], [("ps", 6 + hs)])
            for hs in HS:
                CP("act", WT[:, hsl(hs), :], pw3(3)[:, hsl(hs), :], [("ps", 6 + hs)], [("wt", hs)])
            for hs in HS:
                for h in hrange(hs):
                    MM(pw3(1)[:, h, :], PBF[:, h, :], VB[:, h, :], True, True, [("vb", hs), ("pbf", hs)], [("ps", 2 + hs)])
            for hs in HS:
                CP("act", U[:, hsl(hs), :], pw3(1)[:, hsl(hs), :], [("ps", 2 + hs)], [("xb", hs)])
            stage(15)
            for hs in HS:
                for h in hrange(hs):
                    MM(pw3(2)[:, h, :], KN[:, h, sl], QN[:, h, sl], True, True, [("qkv", 1, h), ("qkv", 0, h)], [("ps", 4 + hs)])
            for hs in HS:
                STT(ATT[:, hsl(hs), :], pw3(2)[:, hsl(hs), :], 128.0 ** -0.5, XA[:, hsl(hs), :], ALU.mult, ALU.mult,
                    [("ps", 4 + hs), ("xa", hs)], [("att", hs)])
            stage(16)
            for chn in range(2):
                r0 = 64 * chn
                rs = slice(r0, r0 + 64)
                for h in range(NH):
                    MM(pw3(3)[:, h, :], WT[:, h, :], SBF[:, h, :], True, True, [("wt", h // 4), ("SBF", h)], [("ps", 6 + h // 4)])
                for hs in HS:
                    TTo("dve", VN[rs, hsl(hs), :], U[rs, hsl(hs), :], pw3(3)[rs, hsl(hs), :], ALU.subtract, [("xb", hs), ("ps", 6 + hs)],
                        [("vn", chn, hs)])
                for h in range(NH):
                    MM(pw3(1)[:, h, :], KD[rs, h, :], VN[rs, h, :], True, True, [("pf", h // 4), ("vn", chn, h // 4)], [("ps", 2 + h // 4)])
                for h in range(NH):
                    MM(pw3(0)[:, h, rs], SBF[:, h, :], QD[:, h, rs], True, False, [("qd", h // 4), ("SBF", h)], [("ps", h // 4)])
                    MM(pw3(0)[:, h, rs], VN[rs, h, :], ATT[rs, h, rs], False, True, [("vn", chn, h // 4), ("att", h // 4)], [("ps", h // 4)])
                for h in range(NH):
                    STT(S[:, h, :], S[:, h, :], EGB[:, h, r0 + 63:r0 + 64], pw3(1)[:, h, :], ALU.mult, ALU.add,
                        [("S", h), ("egb", h // 4), ("ps", 2 + h // 4)], [("S", h)])
                    CP("act" if h % 2 == 0 else "pool", SBF[:, h, :], S[:, h, :], [("S", h)], [("SBF", h)])

        def d_onorm(s):
            sl = subs(s)
            stage(17)
            TK = [("trig", 0), ("trig", 1)]
            WK = [("wt", 0), ("wt", 1)]
            ACT(SQ[:, :, :], pw3(0), AF.Square, [("ps", 0), ("ps", 1)], TK)
            for h in range(NH):
                MM(pw3(2)[:, h, :], ones_b[:, :], SQ[:, h, :], True, True, TK + ["ones_b"], [("ps", 4 + h // 4)])
            ACT(RR[:, :, :], pw3(2), AF.Ln, [("ps", 4), ("ps", 5)], TK, bias=EPS, scale=1.0 / 128)
            ACT(RR[:, :, :], RR[:, :, :], AF.Exp, TK, TK, scale=-0.5)
            STT(T1[:, :, :], pw3(0), dnw[:, 0:1], RR[:, :, :], ALU.mult, ALU.mult, [("ps", 0), ("ps", 1), "dnw"] + TK, WK)
            TTo("dve", OG[:, :, sl], T1[:, :, :], ZS[:, :, sl], ALU.mult, WK + [("zs", c) for c in range(8)], [("og", s)])

        pending_onorm = [None]
        d_prep(0)
        d_head(0)
        for s in range(NSUB):
            d_tail(s)
            if s + 1 < NSUB:
                d_prep(s + 1)
                d_head(s + 1)
                pending_onorm[0] = s
            else:
                d_onorm(s)

        stage(5)
        for g in range(4):
            slot = next_w()
            for j in range(4):
                c = g * 4 + j
                pbk = j % 2
                for kc in range(8):
                    MM(bk(pbk), WR[slot][:, kc, j * 128:(j + 1) * 128], HT[:, kc, :], kc == 0, kc == 7,
                       HTK + [("wr", slot)], [("ps", pbk)])
                ACT(GATES[:, c, :], bk(pbk), AF.Sigmoid, [("ps", pbk), "par"], [*rk_gates(c)],
                    bias=par[:, c % 8, R_BGAT + c // 8:R_BGAT + c // 8 + 1])
            fin()

        stage(21)
        OGK = [("og", s) for s in range(NSUB)]
        for g in range(2):
            slot = next_w()
            for j in range(4):
                c = g * 4 + j
                pbk = j % 2
                for kc in range(8):
                    MM(bk(pbk), WR[slot][:, kc, j * 128:(j + 1) * 128], OG[:, kc, :], kc == 0, kc == 7,
                       OGK + [("wr", slot)], [("ps", pbk)])
                TTo("dve", MRG[:, c, :], bk(pbk), GATES[:, c, :], ALU.mult, [("ps", pbk), *rk_gates(c)], [("zs", c)])
            fin()
        for g in range(2):
            slot = next_w()
            for j in range(4):
                c = g * 4 + j
                pbk = j % 2
                for kc in range(8):
                    MM(bk(pbk), WR[slot][:, kc, j * 128:(j + 1) * 128], CN[:, kc, :], kc == 0, kc == 7,
                       [("cin", k) for k in range(8)] + [("wr", slot)], [("ps", pbk)])
                STT(MA[pbk][:, :], bk(pbk), par[:, c, R_BPW2:R_BPW2 + 1], GATES[:, 8 + c, :], ALU.add, ALU.mult,
                    [("ps", pbk), "par", *rk_gates(8 + c)], [("ma", pbk)])
                TTo("pool", MRG[:, c, :], MRG[:, c, :], MA[pbk][:, :], ALU.add, [("zs", c), ("ma", pbk)], [("zs", c)])
            fin()
        stage(22)
        MK = [("zs", c) for c in range(8)]
        wslots = [next_w(), next_w()]
        for s in range(NSUB):
            for hf in range(2):
                pbk = (2 * s + hf) % 4
                for kc in range(8):
                    MM(bank(pbk), MRG[:, kc, subs(s)], WR[wslots[hf]][:, kc, :], kc == 0, kc == 7,
                       MK + [("wr", wslots[hf])], [("ps", pbk)])
                TTo("dve", XS[s][:, hf * 512:(hf + 1) * 512], XIN[s][:, hf * 512:(hf + 1) * 512], bank(pbk), ALU.add,
                    [("xin", s), ("ps", pbk)], [("xs", s)])
            norm_stats(s, XS[s], ("xs", s))
            if s >= 1:
                norm_apply(s - 1, XS[s - 1], ("xs", s - 1), R_N2, 4)
        fin()
        if t + 1 < NT:
            for s in range(NSUB):
                DMA("sp", XIN[s][:, :], x[tok0 + TT + s * 128:tok0 + TT + (s + 1) * 128, :], (), [("xin", s)], "xl%d" % s)
        stage(23)
        norm_apply(NSUB - 1, XS[NSUB - 1], ("xs", NSUB - 1), R_N2, 4)
        stage(24)
        for g in range(6):
            sg = next_w()
            su = next_w()
            nj = 4 if g < 5 else 2
            for j in range(nj):
                fc = g * 4 + j
                bg_, bu_ = 2 * (j % 2), 2 * (j % 2) + 1
                for kc in range(8):
                    MM(bk(bg_), WR[sg][:, kc, j * 128:(j + 1) * 128], HT[:, kc, :], kc == 0, kc == 7, HTK + [("wr", sg)], [("ps", bg_)])
                for kc in range(8):
                    MM(bk(bu_), WR[su][:, kc, j * 128:(j + 1) * 128], HT[:, kc, :], kc == 0, kc == 7, HTK + [("wr", su)], [("ps", bu_)])
                tw = TMPW[fc % 2]
                ACT(tw[:, :], bk(bg_), AF.Silu, [("ps", bg_)], [("tmpw", fc % 2)])
                TTo("dve", FF[:, fc, :], tw[:, :], bk(bu_), ALU.mult, [("tmpw", fc % 2), ("ps", bu_)], [("qkv", fc // 8, fc % 8)])
            fin()
        stage(25)
        if t + 1 < NT:
            for s in range(NSUB):
                norm_stats(s, XIN[s], ("xin", s))
        FK = [("qkv", fc // 8, fc % 8) for fc in range(22)]
        for hf in range(2):
            for kg in range(3):
                slot = next_w()
                nk = 8 if kg < 2 else 6
                for s in range(NSUB):
                    pbk = 4 + s
                    for kk in range(nk):
                        fc = kg * 8 + kk
                        MM(bank(pbk), FF[:, fc, subs(s)], WR[slot][:, kk, :], fc == 0, fc == 21,
                           [FK[fc], ("wr", slot)], [("ps", pbk)])
                    if kg == 0 and hf == 0 and t + 1 < NT:
                        norm_apply(s, XIN[s], ("xin", s), R_N1, 0)
                    if kg == 2:
                        TTo("dve", XS[s][:, hf * 512:(hf + 1) * 512], XS[s][:, hf * 512:(hf + 1) * 512], bank(pbk), ALU.add,
                            [("xs", s), ("ps", pbk)], [("xs", s)])
                        if hf == 1:
                            hb = HB[s % 2]
                            tok_rstd(XS[s][:, :], hb[:, :], ("hb", s % 2), 4 + s, [("xs", s)], 1.0 / D)
                            STT(XS[s][:, :], XS[s][:, :], STT_[:, 4 + s:5 + s], nfw[:, :], ALU.mult, ALU.mult,
                                [("xs", s), ("st", 4 + s), "nfw"], [("xs", s)])
                            DMA("pool", out[tok0 + s * 128:tok0 + (s + 1) * 128, :], XS[s][:, :], [("xs", s)], [("out", s)],
                                "st%d" % s)
                fin()
        stage(26)
    except _Stop:
        pass
    P.add("pool", lambda e: e.engine_nop(), [("out", s) for s in range(NSUB)], ())
    P.emit(sems, dsems)
    es.close()
    return nc


_CACHE = {}


def kernel(**inputs):
    x = np.ascontiguousarray(inputs["x"], dtype=np.float32)
    B, T, _ = x.shape
    if T not in _CACHE:
        _CACHE[T] = build(T)
    nc = _CACHE[T]
    shared = {}
    for k, v in inputs.items():
        if k == "x":
            continue
        a = np.ascontiguousarray(v, dtype=np.float32)
        if k != "norm_f_w":
            a = a[0]
        shared[k] = np.ascontiguousarray(a)
    in_maps = []
    for b in range(B):
        m = dict(shared)
        m["x"] = x[b]
        in_maps.append(m)
    res = run_bass_kernel_spmd(nc, in_maps, core_ids=list(range(B)))
    return np.stack([np.asarray(r["out"], dtype=np.float32) for r in res.results], axis=0)
```
